# Optimizing a Trainium2 kernel written in Bass

```python
import math
import jax, jax.numpy as jnp
from jax import lax
import numpy as np

D_MODEL = 1024
BATCH = 8
SEQ = 2048
DEPTH = 4

CONV_DIM = 512
CONV_WIDTH = 31
N_GROUPS = 3
HEADS_PER_GROUP = 8
HEAD_DIM = 64
N_HEADS = N_GROUPS * HEADS_PER_GROUP
ATTN_DIM = N_HEADS * HEAD_DIM
ATTN_OUT_DIM = HEADS_PER_GROUP * HEAD_DIM
WINDOWS = (128, 512, 2048)
DILATIONS = (1, 4, 16)
SUB_WINDOW = 128
BLOCK = 128
NUM_BUCKETS = 32
MAX_REL_DISTANCE = 2048
D_FF = 4 * D_MODEL
EPS = 1e-6
NEG_INF = -1e30
IN_COLS = 2 * CONV_DIM + 3 * ATTN_DIM + 2 * D_MODEL

kernel_name = "hybrid_conformer_conv_dilated_attn_gated"


def rms_norm(x, g):
    xf = x.astype(jnp.float32)
    y = xf * lax.rsqrt(jnp.mean(xf * xf, axis=-1, keepdims=True) + EPS)
    return (y * g.astype(jnp.float32)).astype(x.dtype)


def layer_norm(x, g, b):
    xf = x.astype(jnp.float32)
    mu = jnp.mean(xf, axis=-1, keepdims=True)
    xc = xf - mu
    y = xc * lax.rsqrt(jnp.mean(xc * xc, axis=-1, keepdims=True) + EPS)
    return (y * g.astype(jnp.float32) + b.astype(jnp.float32)).astype(x.dtype)


def t5_bucket(dist):
    max_exact = NUM_BUCKETS // 2
    nf = jnp.maximum(dist, 1).astype(jnp.float32)
    large = max_exact + (jnp.log(nf / max_exact) / math.log(MAX_REL_DISTANCE / max_exact)
                         * (NUM_BUCKETS - max_exact)).astype(jnp.int32)
    large = jnp.minimum(large, NUM_BUCKETS - 1)
    return jnp.where(dist < max_exact, dist, large)


def conformer_conv(u, dw_w, dw_b, ln_g, ln_b, w_pw):
    a, gt = jnp.split(u, 2, axis=-1)
    z = a * jax.nn.sigmoid(gt)
    z = lax.conv_general_dilated(
        z, dw_w[:, None, :].astype(z.dtype), window_strides=(1,),
        padding=((CONV_WIDTH - 1, 0),),
        dimension_numbers=('NWC', 'WIO', 'NWC'),
        feature_group_count=CONV_DIM) + dw_b
    z = jax.nn.silu(layer_norm(z, ln_g, ln_b))
    return z @ w_pw


def dilated_group(q, k, v, bias_g, d):
    B, S, H, Dh = q.shape
    L = S // d
    nb = -(-L // BLOCK)
    Lp = nb * BLOCK

    def to_blocks(t):
        t = t.reshape(B, L, d, H, Dh)
        t = jnp.pad(t, ((0, 0), (0, Lp - L), (0, 0), (0, 0), (0, 0)))
        return t.reshape(B, nb, BLOCK, d, H, Dh)

    def band_keys(t):
        prev = jnp.pad(t, ((0, 0), (1, 0), (0, 0), (0, 0), (0, 0), (0, 0)))[:, :-1]
        return jnp.concatenate([prev, t], axis=2)

    qb = to_blocks(q)
    kw = band_keys(to_blocks(k))
    vw = band_keys(to_blocks(v))
    s = jnp.einsum('bnqrhe,bnkrhe->bnrhqk', qb, kw)

    qi = jnp.arange(BLOCK)[:, None]
    kj = jnp.arange(2 * BLOCK)[None, :]
    off = qi + BLOCK - kj
    band = (off >= 0) & (off <= SUB_WINDOW)
    blk = jnp.arange(nb)[:, None, None]
    valid = band[None] & (blk * BLOCK + kj[None] - BLOCK >= 0)
    bucket = t5_bucket(jnp.clip(off, 0, SUB_WINDOW) * d)
    bias = jnp.transpose(bias_g.astype(jnp.float32)[bucket], (2, 0, 1))

    s = jnp.where(valid[None, :, None, None], s + bias, NEG_INF)
    lse = jax.nn.logsumexp(s, axis=-1)
    p = jnp.exp(s - lse[..., None])
    o = jnp.einsum('bnrhqk,bnkrhe->bnqrhe', p, vw)
    o = o.reshape(B, Lp, d, H, Dh)[:, :L].reshape(B, S, H, Dh)
    lse = jnp.transpose(lse, (0, 1, 4, 2, 3)).reshape(B, Lp, d, H)[:, :L].reshape(B, S, H)
    return o, lse


def dilated_attention(qkv, q_g, k_g, rel_bias):
    B, S, _ = qkv.shape
    q, k, v = jnp.split(qkv.astype(jnp.float32), 3, axis=-1)
    shp = (B, S, N_GROUPS, HEADS_PER_GROUP, HEAD_DIM)
    q = rms_norm(q.reshape(shp), q_g) * (HEAD_DIM ** -0.5)
    k = rms_norm(k.reshape(shp), k_g)
    v = v.reshape(shp)
    outs, lses = [], []
    for g in range(N_GROUPS):
        o, l = dilated_group(q[:, :, g], k[:, :, g], v[:, :, g],
                             rel_bias[:, g * HEADS_PER_GROUP:(g + 1) * HEADS_PER_GROUP], DILATIONS[g])
        outs.append(o)
        lses.append(l)
    w = jax.nn.softmax(jnp.stack(lses, axis=0), axis=0)
    o = jnp.sum(w[..., None] * jnp.stack(outs, axis=0), axis=0)
    return o.reshape(B, S, ATTN_OUT_DIM)


def setup_inputs(seed: int = 0) -> dict:
    key = jax.random.key(seed)
    ks = jax.random.split(key, 20)
    f32 = jnp.float32

    def nrm(k, shape, scale):
        return jax.random.normal(k, shape, f32) * scale

    res_scale = (2 * DEPTH) ** -0.5
    return {
        "x": nrm(ks[0], (BATCH, SEQ, D_MODEL), 1.0),
        "rel_bias": nrm(ks[1], (NUM_BUCKETS, N_HEADS), 0.5),
        "norm1_g": 1.0 + nrm(ks[2], (DEPTH, D_MODEL), 0.02),
        "w_in": nrm(ks[3], (DEPTH, D_MODEL, IN_COLS), D_MODEL ** -0.5),
        "q_norm_g": 1.0 + nrm(ks[4], (DEPTH, HEAD_DIM), 0.02),
        "k_norm_g": 1.0 + nrm(ks[5], (DEPTH, HEAD_DIM), 0.02),
        "conv_dw_w": nrm(ks[6], (DEPTH, CONV_WIDTH, CONV_DIM), CONV_WIDTH ** -0.5),
        "conv_dw_b": nrm(ks[7], (DEPTH, CONV_DIM), 0.02),
        "conv_ln_g": 1.0 + nrm(ks[8], (DEPTH, CONV_DIM), 0.02),
        "conv_ln_b": nrm(ks[9], (DEPTH, CONV_DIM), 0.02),
        "w_conv_out": nrm(ks[10], (DEPTH, CONV_DIM, D_MODEL), CONV_DIM ** -0.5),
        "w_attn_out": nrm(ks[11], (DEPTH, ATTN_OUT_DIM, D_MODEL), ATTN_OUT_DIM ** -0.5),
        "w_out": nrm(ks[12], (DEPTH, D_MODEL, D_MODEL), D_MODEL ** -0.5 * res_scale),
        "norm2_g": 1.0 + nrm(ks[13], (DEPTH, D_MODEL), 0.02),
        "w_ff1": nrm(ks[14], (DEPTH, D_MODEL, D_FF), D_MODEL ** -0.5),
        "w_ff2": nrm(ks[15], (DEPTH, D_FF, D_MODEL), D_FF ** -0.5 * res_scale),
    }


def reference(x, rel_bias, norm1_g, w_in, q_norm_g, k_norm_g, conv_dw_w, conv_dw_b,
              conv_ln_g, conv_ln_b, w_conv_out, w_attn_out, w_out, norm2_g, w_ff1, w_ff2):
    c_conv = 2 * CONV_DIM
    c_attn = c_conv + 3 * ATTN_DIM
    for l in range(DEPTH):
        h = rms_norm(x, norm1_g[l])
        u = h @ w_in[l]
        y_conv = conformer_conv(u[..., :c_conv], conv_dw_w[l], conv_dw_b[l],
                                conv_ln_g[l], conv_ln_b[l], w_conv_out[l])
        y_attn = dilated_attention(u[..., c_conv:c_attn], q_norm_g[l], k_norm_g[l],
                                   rel_bias).astype(x.dtype) @ w_attn_out[l]
        g_conv, g_attn = jnp.split(jax.nn.sigmoid(u[..., c_attn:]), 2, axis=-1)
        x = x + (g_conv * y_conv + g_attn * y_attn) @ w_out[l]
        h = rms_norm(x, norm2_g[l])
        x = x + jnp.square(jax.nn.relu(h @ w_ff1[l])) @ w_ff2[l]
    return x
```

```python
from contextlib import ExitStack

import numpy as np
import concourse.bass as bass
import concourse.mybir as mybir
from concourse.bass_utils import run_bass_kernel_spmd

F32 = mybir.dt.float32
BF16 = mybir.dt.bfloat16
AF = mybir.ActivationFunctionType
ALU = mybir.AluOpType

D_MODEL = 1024
SEQ = 2048
DEPTH = 4
NCORES = 8
EPS = 1e-6
NT = 4
TOK = 512
KC = 8
DIL = (1, 4, 16)
NPAR = 154
RING_N = 5
RING_SZ = 2048


class Prog:
    STREAMS = ("pe", "act", "dve", "pool", "sp")
    NOSELF = ("pe", "sp")

    def __init__(self):
        self.streams = {s: [] for s in self.STREAMS}
        self.last_w = {}
        self.readers = {}
        self.dma_domains = []
        self.pending = {}

    def op(self, stream, fn, r=(), w=(), dma=None):
        ops = self.streams[stream]
        me = (stream, len(ops))
        deps = set()
        for res in r:
            lw = self.last_w.get(res)
            if lw is not None:
                deps.add(lw)
        for res in w:
            lw = self.last_w.get(res)
            if lw is not None:
                deps.add(lw)
            for rd in self.readers.get(res, ()):
                deps.add(rd)
        pend = self.pending.pop(stream, None)
        if pend:
            deps |= pend
        deps.discard(me)
        best = {}
        red = set()
        for (ps_, pi_) in deps:
            if self.streams[ps_][pi_]["dma"] is not None:
                red.add((ps_, pi_))
            elif pi_ > best.get(ps_, -1):
                best[ps_] = pi_
        for ps_, pi_ in best.items():
            red.add((ps_, pi_))
        deps = red
        for res in r:
            self.readers.setdefault(res, []).append(me)
        for res in w:
            self.last_w[res] = me
            self.readers[res] = []
        if dma is not None and dma not in self.dma_domains:
            self.dma_domains.append(dma)
        ops.append(dict(fn=fn, deps=deps, dma=dma, sig=False, val=None))
        return me

    def barrier(self, streams=("pe", "act", "dve")):
        pend = set()
        for s in streams:
            n = len(self.streams[s])
            if n:
                pend.add((s, n - 1))
        for s in streams:
            self.pending[s] = set(pend) | self.pending.get(s, set())

    def resolve(self):
        for s, ops in self.streams.items():
            for o in ops:
                for (ps, pi) in o["deps"]:
                    p = self.streams[ps][pi]
                    if p["dma"] is None:
                        if ps == s and s in self.NOSELF:
                            continue
                        p["sig"] = True
        dcount = {}
        for s, ops in self.streams.items():
            cnt = 0
            for o in ops:
                if o["dma"] is not None:
                    d = o["dma"]
                    dcount[d] = dcount.get(d, 0) + 1
                    o["val"] = 16 * dcount[d]
                elif o["sig"]:
                    cnt += 1
                    o["val"] = cnt

    def emit(self, sems, block):
        self.resolve()
        prog = self

        def mk(stream):
            def body(eng):
                waited = {}
                for o in prog.streams[stream]:
                    need = {}
                    for (ps, pi) in o["deps"]:
                        p = prog.streams[ps][pi]
                        if p["dma"] is not None:
                            dom = p["dma"]
                        else:
                            if ps == stream and stream in prog.NOSELF:
                                continue
                            dom = ps
                        v = p["val"]
                        if v > need.get(dom, 0):
                            need[dom] = v
                    for dom, v in need.items():
                        if v > waited.get(dom, 0):
                            eng.wait_ge(sems[dom], v)
                            waited[dom] = v
                    ins = o["fn"](eng)
                    if o["dma"] is not None:
                        ins.then_inc(sems[o["dma"]], 16)
                    elif o["sig"]:
                        ins.then_inc(sems[stream], 1)
            return body

        block.tensor(mk("pe"))
        block.scalar(mk("act"))
        block.vector(mk("dve"))
        block.gpsimd(mk("pool"))
        block.sync(mk("sp"))


def _tile_list():
    tl = []
    for c in range(4):
        tl.append((("cv", c), 2048))
    for pi in range(4):
        for g in range(3):
            tl.append((("qk", pi, g), 2048))
            tl.append((("v", pi, g), 1024))
    for f in range(8):
        tl.append((("g", f), 2048))
        tl.append((("wc", f), 1024))
    for f in range(8):
        tl.append((("wo", f), 1024))
    for qd in range(4):
        for cc in range(4):
            tl.append((("f1", qd, cc), 2048))
        for jj in range(4):
            tl.append((("f2", qd, jj), 2048))
    return tl


TILES = _tile_list()
TILE_OFF = {}
_o = 0
for _k, _s in TILES:
    TILE_OFF[_k] = (_o, _s)
    _o += _s
WTOT = _o


def _kmaj(w):
    K, N = w.shape
    return w.reshape(K // 128, 128, N).transpose(1, 0, 2)


def pack_layer(inp, l):
    w_in = inp["w_in"][l]
    out = np.empty((128, WTOT), np.float32)

    def put(key, arr):
        o, s = TILE_OFF[key]
        out[:, o:o + s] = arr.reshape(128, s)

    for c in range(4):
        cols = np.r_[c * 128:(c + 1) * 128, 512 + c * 128:512 + (c + 1) * 128]
        put(("cv", c), _kmaj(w_in[:, cols]))
    for pi in range(4):
        for g in range(3):
            h0 = g * 8 + 2 * pi
            qc = 1024 + h0 * 64
            kc_ = 1024 + 1536 + h0 * 64
            vc = 1024 + 3072 + h0 * 64
            cols = np.r_[qc:qc + 128, kc_:kc_ + 128]
            put(("qk", pi, g), _kmaj(w_in[:, cols]))
            put(("v", pi, g), _kmaj(w_in[:, vc:vc + 128]))
    wco = inp["w_conv_out"][l]
    wao = inp["w_attn_out"][l]
    for f in range(8):
        cols = np.r_[5632 + f * 128:5632 + (f + 1) * 128, 6656 + f * 128:6656 + (f + 1) * 128]
        put(("g", f), _kmaj(w_in[:, cols]))
        a = _kmaj(wco[:, f * 128:(f + 1) * 128])
        b = _kmaj(wao[:, f * 128:(f + 1) * 128])
        put(("wc", f), np.concatenate([a, b], axis=2))
    wo = inp["w_out"][l]
    for f in range(8):
        put(("wo", f), _kmaj(wo[:, f * 128:(f + 1) * 128]))
    w1 = inp["w_ff1"][l]
    w2 = inp["w_ff2"][l]
    for qd in range(4):
        for cc in range(4):
            c0 = (qd * 8 + cc * 2) * 128
            put(("f1", qd, cc), _kmaj(w1[:, c0:c0 + 256]))
        rows = w2[qd * 1024:(qd + 1) * 1024]
        for jj in range(4):
            t = _kmaj(rows[:, jj * 256:(jj + 1) * 256])
            t = t.reshape(128, 8, 2, 128).transpose(0, 2, 1, 3)
            put(("f2", qd, jj), t)
    return out


def pack_params(inp, l):
    def fm(v, n):
        return v.reshape(n, 128).T
    cols = [
        fm(inp["norm1_g"][l], 8), fm(inp["norm2_g"][l], 8),
        inp["conv_dw_w"][l].reshape(31, 4, 128).transpose(2, 1, 0).reshape(128, 124),
        fm(inp["conv_dw_b"][l], 4), fm(inp["conv_ln_g"][l], 4), fm(inp["conv_ln_b"][l], 4),
        np.tile(inp["q_norm_g"][l], 2)[:, None], np.tile(inp["k_norm_g"][l], 2)[:, None],
    ]
    return np.concatenate(cols, axis=1).astype(np.float32)


P_G1, P_G2, P_DWW, P_DWB, P_LNG, P_LNB, P_GQ, P_GK = 0, 8, 16, 140, 144, 148, 152, 153


def _t5_bucket(dist):
    dist = np.asarray(dist, np.int64)
    nf = np.maximum(dist, 1).astype(np.float32)
    large = 16 + (np.log(nf / np.float32(16)) / np.float32(np.log(128.0)) * np.float32(16)).astype(np.int32)
    large = np.minimum(large, 31)
    return np.where(dist < 16, dist, large)


def bias_tables(rel_bias):
    k = np.arange(128)[:, None]
    q = np.arange(128)[None, :]
    off_prev = q + 128 - k
    off_cur = q - k
    m_prev = (off_prev <= 128).astype(np.float32)
    m_cur = (off_cur >= 0).astype(np.float32)
    bias = np.zeros((128, 12, 4, 128), np.float32)
    mask = np.zeros((128, 12, 4, 128), np.float32)
    for g in range(3):
        d = DIL[g]
        b_prev = _t5_bucket(np.clip(off_prev, 0, 128) * d)
        b_cur = _t5_bucket(np.clip(off_cur, 0, 128) * d)
        for pi in range(4):
            for hh in range(2):
                h = g * 8 + 2 * pi + hh
                bias[:, g * 4 + pi, 2 * hh + 0, :] = rel_bias[b_prev, h]
                bias[:, g * 4 + pi, 2 * hh + 1, :] = rel_bias[b_cur, h]
                mask[:, g * 4 + pi, 2 * hh + 0, :] = m_prev
                mask[:, g * 4 + pi, 2 * hh + 1, :] = m_cur
    return bias.reshape(128, 6144), mask.reshape(128, 6144)


def const_table():
    c = np.zeros((128, 512), np.float32)
    c[:, 0:128] = np.eye(128)
    c[0:64, 128:192] = 1.0
    c[64:128, 192:256] = 1.0
    c[:, 256:320] = 1.0
    c[:, 448:512] = 1.0
    return c


def tok_slice(g, bi):
    if g == 0:
        return slice(bi * 128, bi * 128 + 128, 1)
    if g == 1:
        r, n = bi // 4, bi % 4
        s = 512 * n + r
        return slice(s, s + 4 * 127 + 1, 4)
    return slice(bi, bi + 16 * 127 + 1, 16)


def has_prev(g, bi):
    if g == 0:
        return bi > 0
    if g == 1:
        return (bi % 4) > 0
    return False


def tok_chunks(g, bi):
    if g == 0:
        return [bi // 4]
    if g == 1:
        return [bi % 4]
    return [0, 1, 2, 3]


def build_nc(L, dbg=None):
    nc = bass.Bass("TRN2", target_bir_lowering=False)
    xin = nc.dram_tensor("xin", [128, KC * SEQ], F32, kind="ExternalInput").ap()
    wl = nc.dram_tensor("wl", [L, 128, WTOT], F32, kind="ExternalInput").ap()
    par = nc.dram_tensor("par", [128, L * NPAR], F32, kind="ExternalInput").ap()
    btab = nc.dram_tensor("btab", [128, 6144], F32, kind="ExternalInput").ap()
    mtab = nc.dram_tensor("mtab", [128, 6144], F32, kind="ExternalInput").ap()
    cst = nc.dram_tensor("cst", [128, 512], F32, kind="ExternalInput").ap()
    yout = nc.dram_tensor("yout", [128, KC * SEQ], F32, kind="ExternalOutput").ap()
    dbg_out = None
    if dbg is not None:
        dbg_out = nc.dram_tensor("dbg", [128, dbg[1]], F32, kind="ExternalOutput").ap()

    with ExitStack() as es:
        def sb(name, shape, dt):
            return es.enter_context(nc.sbuf_tensor(name, shape, dt))

        xT_t = sb("xT", [128, KC * SEQ], F32)
        hT_t = sb("hT", [128, KC * SEQ], BF16)
        oT_t = sb("oT", [128, 4 * SEQ], BF16)
        ring_t = sb("ring", [128, RING_N * RING_SZ], BF16)
        expb_t = sb("expb", [128, 6144], BF16)
        cst_t = sb("cstb", [128, 512], BF16)
        onesb_t = sb("onesb", [128, 128], BF16)
        onesm_t = sb("onesm", [128, 128], BF16)
        par_t = sb("par_sb", [128, L * NPAR], F32)
        npar_t = sb("npar_sb", [128, L * 8], F32)
        junk_t = sb("junk", [128, 8], F32)
        junk2_t = sb("junk2", [128, 8 * RING_N], F32)
        ARENA_W = 14592
        arena = sb("arena", [128, ARENA_W], F32)
        banks = [es.enter_context(nc.psum_tensor(f"bank{i}", [128, 512], F32)) for i in range(8)]

        P = Prog()
        sems = {}
        dma_doms = [f"ring{i}" for i in range(RING_N)] + [f"free{i}" for i in range(RING_N)] + ["ld_x0", "ld_x1", "ld_x2", "ld_x3", "ld_par", "ld_b", "ld_m", "ld_c", "st", "dbg", "shf", "shf2"]
        for d in list(Prog.STREAMS) + dma_doms:
            sems[d] = es.enter_context(nc.semaphore("s_" + d))
        blk = es.enter_context(nc.Block())

        xT = xT_t[:].rearrange("p (k t) -> p k t", k=KC)
        hT = hT_t[:].rearrange("p (k t) -> p k t", k=KC)
        oT = oT_t[:].rearrange("p (k t) -> p k t", k=4)
        ident = cst_t[:, 0:128]
        bdiag = cst_t[:, 128:256]
        onesA = cst_t[:, 256:384]
        onesB = cst_t[:, 384:512]
        onesb = onesb_t[:]
        onesm = onesm_t[:]
        expb = expb_t[:].rearrange("p (a t q) -> p a t q", a=12, t=4)

        def af32(off, n):
            return arena[:, off // 4: off // 4 + n]

        def ab16(off, n):
            return arena[:, off // 4: off // 4 + n // 2].bitcast(BF16)

        bstate = {"i": 0, "n": 8}

        def nbank():
            b = bstate["i"] % bstate["n"]
            bstate["i"] += 1
            return b

        def MM(out, lhsT, rhs, start, stop, r, w):
            P.op("pe", lambda e: e.matmul(out, lhsT=lhsT, rhs=rhs, start=start, stop=stop), r=r, w=w)

        def ACT(out, in_, func, r, w, scale=1.0, bias=0.0):
            P.op("act", lambda e: e.activation(out=out, in_=in_, func=func, scale=scale, bias=bias), r=r, w=w)

        def TT(eng, out, in0, in1, op, r, w):
            P.op(eng, lambda e: e.tensor_tensor(out=out, in0=in0, in1=in1, op=op), r=r, w=w)

        def STT(out, in0, scalar, in1, op0, op1, r, w):
            P.op("dve", lambda e: e.scalar_tensor_tensor(out=out, in0=in0, scalar=scalar, in1=in1,
                                                         op0=op0, op1=op1), r=r, w=w)

        def TS(eng, out, in0, s1, op0, r, w, s2=None, op1=None):
            if op1 is None:
                P.op(eng, lambda e: e.tensor_scalar(out=out, in0=in0, scalar1=s1, scalar2=None, op0=op0), r=r, w=w)
            else:
                P.op(eng, lambda e: e.tensor_scalar(out=out, in0=in0, scalar1=s1, scalar2=s2, op0=op0, op1=op1),
                     r=r, w=w)

        def CP(eng, out, in_, r, w):
            if eng == "act":
                P.op(eng, lambda e: e.activation(out=out, in_=in_, func=AF.Identity), r=r, w=w)
            else:
                P.op(eng, lambda e: e.tensor_copy(out=out, in_=in_), r=r, w=w)

        def MEMSET(eng, ap, val, w):
            P.op(eng, lambda e: e.memset(ap, val), w=w)

        ring_state = {"i": 0}

        def wload(l, key):
            s = ring_state["i"] % RING_N
            ring_state["i"] += 1
            off, size = TILE_OFF[key]
            dst = ring_t[:, s * RING_SZ: s * RING_SZ + size]
            src = wl[l, :, off:off + size]
            if P.readers.get(("ring", s)):
                P.op("sp", lambda e: e.dma_start(out=junk2_t[:, s * 8:(s + 1) * 8], in_=junk_t[:, :]), r=["junk"],
                     w=[("ring", s)], dma=f"free{s}")
            P.op("pool", lambda e: e.dma_start(out=dst, in_=src), w=[("ring", s)], dma=f"ring{s}")
            return dst, ("ring", s)

        class WStream:
            def __init__(self, seq):
                self.seq = seq
                self.pos = 0
                self.head = 0
                self.loaded = {}
                self.busy = [None] * RING_N
                self.key2idx = {}

            def _fill(self):
                while self.pos < len(self.seq):
                    s_ = self.pos % RING_N
                    if self.busy[s_] is not None:
                        break
                    l, key = self.seq[self.pos]
                    self.busy[s_] = self.pos
                    self.loaded[self.pos] = wload(l, key)
                    self.pos += 1

            def get(self, key):
                self._fill()
                i = self.head
                assert self.seq[i][1] == key, (self.seq[i], key)
                assert i in self.loaded, "ring too small for outstanding tiles"
                self.head += 1
                self.key2idx[key] = i
                return self.loaded.pop(i)

            def done(self, key):
                i = self.key2idx.pop(key)
                s_ = i % RING_N
                assert self.busy[s_] == i
                self.busy[s_] = None
                self._fill()

        wseq = []
        for l in range(L):
            for pi in range(4):
                for g in range(3):
                    wseq.append((l, ("qk", pi, g)))
                    wseq.append((l, ("v", pi, g)))
            for c in range(4):
                wseq.append((l, ("cv", c)))
            for th in range(2):
                for f in range(8):
                    wseq.append((l, ("g", f)))
                    wseq.append((l, ("wc", f)))
                for f in range(8):
                    wseq.append((l, ("wo", f)))
            for qd in range(4):
                for cc in range(4):
                    wseq.append((l, ("f1", qd, cc)))
                for rep in range(2 if qd == 3 else 1):
                    for jj in range(4):
                        wseq.append((l, ("f2", qd, jj)))
        WS = WStream(wseq)

        for i in range(4):
            P.op("sp", lambda e, i=i: e.dma_start(out=xT_t[:, i * 4096:(i + 1) * 4096],
                                                   in_=xin[:, i * 4096:(i + 1) * 4096]),
                 w=[("xT", kc, n) for kc in (2 * i, 2 * i + 1) for n in range(NT)], dma=f"ld_x{i}")
        P.op("sp", lambda e: e.dma_start(out=par_t[:], in_=par[:, :]), w=["par"], dma="ld_par")
        P.op("pool", lambda e: e.dma_start(out=cst_t[:], in_=cst[:, :]), w=["cst"], dma="ld_c")
        bt = af32(0, 6144)
        mt = af32(24576, 6144)
        P.op("sp", lambda e: e.dma_start(out=bt, in_=btab[:, :]), w=["bt"], dma="ld_b")
        P.op("sp", lambda e: e.dma_start(out=mt, in_=mtab[:, :]), w=["mt"], dma="ld_m")
        MEMSET("dve", onesb, 1.0, w=["onesb"])
        MEMSET("dve", junk_t[:], 0.0, w=["junk"])
        MEMSET("dve", onesm, 1.0 / 512.0, w=["onesm"])
        for i in range(3):
            sl_ = slice(i * 2048, (i + 1) * 2048)
            TS("dve", mt[:, sl_], mt[:, sl_], 240000.0, ALU.mult, r=["mt"], w=["mt"], s2=-240000.0, op1=ALU.add)
            STT(expb_t[:, sl_], bt[:, sl_], 8.0, mt[:, sl_], ALU.mult, ALU.add, r=["bt", "mt"], w=["expb"])
        for l in range(L):
            TS("dve", npar_t[:, l * 8:l * 8 + 8], par_t[:, l * NPAR + P_LNG: l * NPAR + P_LNG + 8], -1.0,
               ALU.mult, r=["par"], w=["npar"])
        P.barrier()

        def rmsnorm(l, gcol, first, chunks_=(0, 1, 2, 3)):
            sq = ab16(45056, 4096).rearrange("p (k t) -> p k t", k=KC)
            lnv = af32(53248, 512)
            rstd = af32(55296, 512)
            for n in chunks_:
                ts = slice(n * TOK, (n + 1) * TOK)
                xr = [("xT", kc, n) for kc in range(KC)]
                ACT(sq, xT[:, :, ts], AF.Square, r=xr, w=["n_sq"])
                b = nbank()
                for kc in range(KC):
                    MM(banks[b][:], onesb, sq[:, kc, :], kc == 0, kc == KC - 1, r=["n_sq", "onesb"], w=[("ps", b)])
                ACT(lnv, banks[b][:], AF.Ln, r=[("ps", b)], w=["n_ln"], scale=1.0 / D_MODEL, bias=EPS)
                ACT(rstd, lnv, AF.Exp, r=["n_ln"], w=["n_rstd"], scale=-0.5)
                for kc in range(KC):
                    STT(hT[:, kc, ts], xT[:, kc, ts], par_t[:, l * NPAR + gcol + kc: l * NPAR + gcol + kc + 1], rstd,
                        ALU.mult, ALU.mult, r=[("xT", kc, n), "n_rstd", "par"],
                        w=[("hT", kc, n)])

        def attention(l):
            bstate["n"] = 4
            qpA = ab16(0, 2048)
            qpB = ab16(4096, 2048)
            kT = ab16(8192, 2048)
            VpA = ab16(12288, 2048).rearrange("p (b d) -> p b d", b=16)
            VpB = ab16(16384, 2048).rearrange("p (b d) -> p b d", b=16)
            numacc = af32(20480, 2048)
            denacc = af32(28672, 2048)
            S0 = 36864
            uqs = [af32(S0 + 2048 * i, 512) for i in range(3)]
            sqb = [ab16(S0 + 6144 + 1024 * i, 512) for i in range(3)]
            lnv = [af32(S0 + 9216 + 2048 * i, 512) for i in range(3)]
            rstd = lnv
            eb = [ab16(S0 + 15360 + 1024 * i, 512) for i in range(2)]
            pTb = [ab16(S0 + 17408 + 1024 * i, 512) for i in range(4)]
            MEMSET("dve", qpA[64:128, :], 0.0, w=["qpA_z"])
            MEMSET("dve", qpB[0:64, :], 0.0, w=["qpB_z"])
            MEMSET("dve", VpA[:, :, 64:128], 1.0, w=["VpA_z"])
            MEMSET("dve", VpB[:, :, 0:64], 1.0, w=["VpB_z"])
            import os as _os2
            ksub = int(_os2.environ.get("KSUB", "99"))
            cnt = {"u": 0, "e": 0, "p": 0}
            gq = par_t[:, l * NPAR + P_GQ: l * NPAR + P_GQ + 1]
            gk = par_t[:, l * NPAR + P_GK: l * NPAR + P_GK + 1]
            TMPRES = [("a_uq", i) for i in range(3)] + [("a_sq", i) for i in range(3)]
            NACC = [("numacc", q_) for q_ in range(4)]
            DACC = [("denacc", q_) for q_ in range(4)]
            tmpd = af32(S0, 2048)

            def normalise_dma(pj):
                ia = P.op("sp", lambda e: e.dma_start(out=tmpd[0:64, :], in_=numacc[64:128, :]), r=NACC + DACC,
                          w=TMPRES + [("tmpd", q_) for q_ in range(4)], dma="shf")
                depsA = set(P.streams["sp"][ia[1]]["deps"])
                ib = P.op("sp", lambda e: e.dma_start(out=tmpd[64:128, :], in_=denacc[0:64, :]), r=[],
                          w=[("tmpdB", q_) for q_ in range(4)], dma="shf2")
                P.streams["sp"][ib[1]]["deps"] |= depsA

            def normalise_chunk(pj, q_):
                cs = slice(q_ * 512, (q_ + 1) * 512)
                tr = [("tmpd", q_), ("tmpdB", q_)]
                ACT(tmpd[:, cs], tmpd[:, cs], AF.Ln, r=tr + TMPRES, w=tr)
                ACT(tmpd[:, cs], tmpd[:, cs], AF.Exp, r=tr + TMPRES, w=tr, scale=-1.0)
                TT("dve", oT[0:64, pj, cs], numacc[0:64, cs], tmpd[0:64, cs], ALU.mult,
                   r=[("numacc", q_)] + tr + TMPRES, w=[("oT", pj)])
                TT("dve", oT[64:128, pj, cs], denacc[64:128, cs], tmpd[64:128, cs], ALU.mult,
                   r=[("denacc", q_)] + tr + TMPRES, w=[("oT", pj)])

            pend_norm = {"pi": None}
            kpg = int(_os2.environ.get("KPG", "99"))
            for pi in range(4):
                for g in range(3):
                    if pi * 3 + g >= kpg:
                        WS.get(("qk", pi, g)); WS.get(("v", pi, g))
                        WS.done(("qk", pi, g)); WS.done(("v", pi, g))
                        continue
                    wqk, rqk = WS.get(("qk", pi, g))
                    wv, rv = WS.get(("v", pi, g))
                    wqk3 = wqk.rearrange("p (k c) -> p k c", k=KC)
                    wv3 = wv.rearrange("p (k c) -> p k c", k=KC)
                    chunks = [(which, n) for which in range(2) for n in range(NT)]
                    cst_ = {}

                    def stageP(ci):
                        which, n = chunks[ci]
                        ts = slice(n * TOK, (n + 1) * TOK)
                        b = nbank()
                        for kc in range(KC):
                            MM(banks[b][:], wqk3[:, kc, which * 128:(which + 1) * 128], hT[:, kc, ts],
                               kc == 0, kc == KC - 1, r=[rqk, ("hT", kc, n)], w=[("ps", b)])
                        u = cnt["u"] % 3
                        cnt["u"] += 1
                        cst_[ci] = u
                        CP("dve", uqs[u], banks[b][:], r=[("ps", b)], w=[("a_uq", u)])
                        ACT(sqb[u], uqs[u], AF.Square, r=[("a_uq", u)], w=[("a_sq", u)])

                    def stageN(ci):
                        which, n = chunks[ci]
                        ts = slice(n * TOK, (n + 1) * TOK)
                        u = cst_[ci]
                        b2 = nbank()
                        MM(banks[b2][:], bdiag, sqb[u], True, True, r=[("a_sq", u), "cst"], w=[("ps", b2)])
                        ACT(lnv[u], banks[b2][:], AF.Ln, r=[("ps", b2)], w=[("a_rstd", u)], scale=1.0 / 64.0, bias=EPS)
                        ACT(rstd[u], lnv[u], AF.Exp, r=[("a_rstd", u)], w=[("a_rstd", u)], scale=-0.5)
                        if which == 0:
                            STT(qpA[0:64, ts], uqs[u][0:64, :], gq[0:64, :], rstd[u][0:64, :], ALU.mult, ALU.mult,
                                r=[("a_uq", u), ("a_rstd", u), "par", "qpA_z"], w=[("qpA", n)])
                            STT(qpB[64:128, ts], uqs[u][64:128, :], gq[64:128, :], rstd[u][64:128, :],
                                ALU.mult, ALU.mult,
                                r=[("a_uq", u), ("a_rstd", u), "par", "qpB_z"], w=[("qpB", n)])
                        else:
                            STT(kT[:, ts], uqs[u], gk, rstd[u], ALU.mult, ALU.mult,
                                r=[("a_uq", u), ("a_rstd", u), "par"], w=[("kT", n)])

                    def vproj(b4):
                        if True:
                            b = nbank()
                            for j in range(4):
                                bi = b4 * 4 + j
                                sl = tok_slice(g, bi)
                                for kc in range(KC):
                                    MM(banks[b][:, j * 128:(j + 1) * 128], hT[:, kc, sl], wv3[:, kc, :],
                                       kc == 0, kc == KC - 1,
                                       r=[rv] + [("hT", kc, n) for n in tok_chunks(g, bi)], w=[("ps", b)])
                            pv = banks[b][:].rearrange("p (j d) -> p j d", j=4)
                            CP("dve", VpA[:, b4 * 4:(b4 + 1) * 4, 0:64], pv[:, :, 0:64], r=[("ps", b), "VpA_z"],
                               w=[("VpA", b4)])
                            CP("dve", VpB[:, b4 * 4:(b4 + 1) * 4, 64:128], pv[:, :, 64:128], r=[("ps", b), "VpB_z"],
                               w=[("VpB", b4)])

                    stageP(0)
                    stageP(1)
                    for ci in range(8):
                        if ci + 2 < 8:
                            stageP(ci + 2)
                        if 1 <= ci <= 4:
                            vproj(ci - 1)
                        if ci == 5:
                            WS.done(("qk", pi, g))
                            WS.done(("v", pi, g))
                        stageN(ci)

                    sst = {}

                    def stageS(bi):
                        hp = has_prev(g, bi)
                        qs = tok_slice(g, bi)
                        qch = tok_chunks(g, bi)
                        kbl = ([bi - 1] if hp else []) + [bi]
                        ntile = 2 * len(kbl)
                        sbk = nbank()
                        W = ntile * 128
                        if hp:
                            MM(banks[sbk][:, 0:W], ident, expb[:, g * 4 + pi, :, :].rearrange("p t q -> p (t q)"),
                               True, False, r=["cst", "expb"], w=[("ps", sbk)])
                        else:
                            MM(banks[sbk][:, 0:W].rearrange("p (t q) -> p t q", t=2), ident,
                               expb[:, g * 4 + pi, 1::2, :], True, False, r=["cst", "expb"], w=[("ps", sbk)])
                        t = 0
                        for hh in range(2):
                            qp = qpA if hh == 0 else qpB
                            qn = "qpA" if hh == 0 else "qpB"
                            for kb in kbl:
                                MM(banks[sbk][:, t * 128:(t + 1) * 128], kT[:, tok_slice(g, kb)], qp[:, qs],
                                   False, t == ntile - 1,
                                   r=[("kT", n) for n in tok_chunks(g, kb)] + [(qn, n) for n in qch],
                                   w=[("ps", sbk)])
                                t += 1
                        W = ntile * 128
                        pidx = cnt["p"] % 4
                        cnt["p"] += 1
                        pt = pTb[pidx]
                        ACT(pt[:, 0:W], banks[sbk][:, 0:W], AF.Exp, r=[("ps", sbk)], w=[("a_p", pidx)], scale=0.125)
                        sst[bi] = (kbl, pidx)

                    def stageV(bi):
                        u4, j = bi // 4, bi % 4
                        nb_, db_ = 4 + 2 * (u4 % 2), 5 + 2 * (u4 % 2)
                        kbl, pidx = sst.pop(bi)
                        pt = pTb[pidx]
                        nk = len(kbl)
                        for hh in range(2):
                            Vp = VpA if hh == 0 else VpB
                            vn = "VpA" if hh == 0 else "VpB"
                            bk = nb_ if hh == 0 else db_
                            for ki, kb in enumerate(kbl):
                                MM(banks[bk][:, j * 128:(j + 1) * 128], Vp[:, kb, :],
                                   pt[:, (hh * nk + ki) * 128:(hh * nk + ki + 1) * 128], ki == 0, ki == nk - 1,
                                   r=[(vn, kb // 4), ("a_p", pidx)], w=[("ps", bk)])
                        if j != 3:
                            return
                        if g == 0:
                            nv = numacc[:, u4 * 512:(u4 + 1) * 512]
                            dv = denacc[:, u4 * 512:(u4 + 1) * 512]
                            pn, pd = banks[nb_][:], banks[db_][:]
                        elif g == 1:
                            nv = numacc[:, u4::4]
                            dv = denacc[:, u4::4]
                            pn, pd = banks[nb_][:], banks[db_][:]
                        else:
                            nv = numacc[:, :].rearrange("p (q r) -> p r q", r=16)[:, u4 * 4:(u4 + 1) * 4, :]
                            dv = denacc[:, :].rearrange("p (q r) -> p r q", r=16)[:, u4 * 4:(u4 + 1) * 4, :]
                            pn = banks[nb_][:].rearrange("p (r q) -> p r q", r=4)
                            pd = banks[db_][:].rearrange("p (r q) -> p r q", r=4)
                        if g == 0:
                            if pend_norm["pi"] is not None:
                                normalise_chunk(pend_norm["pi"], u4)
                                if u4 == 3:
                                    pend_norm["pi"] = None
                            CP("act", nv, pn, r=[("ps", nb_)], w=[("numacc", u4)])
                            CP("dve", dv, pd, r=[("ps", db_)], w=[("denacc", u4)])
                        else:
                            TT("dve", nv, pn, nv, ALU.add, r=[("ps", nb_)] + NACC, w=NACC)
                            TT("dve", dv, pd, dv, ALU.add, r=[("ps", db_)] + DACC, w=DACC)

                    if pend_norm["pi"] is not None:
                        normalise_dma(pend_norm["pi"])
                    stageS(0)
                    stageS(1)
                    for bi in range(16):
                        if bi + 2 < 16:
                            stageS(bi + 2)
                        stageV(bi)
                pend_norm["pi"] = pi
            if pend_norm["pi"] is not None:
                normalise_dma(pend_norm["pi"])
                for q_ in range(4):
                    normalise_chunk(pend_norm["pi"], q_)
                pend_norm["pi"] = None
            bstate["n"] = 8

        def convbranch(l):
            cT = ab16(0, 8192).rearrange("p (c t) -> p c t", c=4)
            ZW = 2080
            zpad = [ab16(16384, ZW), ab16(16384 + 4160, ZW)]
            Dm = [ab16(24832, 31 * 128).rearrange("p (j m) -> p j m", j=31),
                  ab16(24832 + 7936, 31 * 128).rearrange("p (j m) -> p j m", j=31)]
            G0 = 40704
            ge = [af32(G0, 512), af32(G0 + 2048, 512)]
            gs = [af32(G0 + 4096, 512), af32(G0 + 6144, 512)]
            pc = l * NPAR
            for i in range(2):
                MEMSET("dve", zpad[i][:, 0:30], 0.0, w=[("zpad_z", i)])
            k = 0
            for c in range(4):
                wcv, rcv = WS.get(("cv", c))
                wcv3 = wcv.rearrange("p (k c) -> p k c", k=KC)
                zp = zpad[c % 2]
                D = Dm[c % 2]

                def buildD(cc_):
                    for j in range(31):
                        ACT(Dm[cc_ % 2][:, j, :], ident, AF.Identity, r=["cst", "par"], w=[("D", cc_ % 2)],
                            scale=par_t[:, pc + P_DWW + cc_ * 31 + j: pc + P_DWW + cc_ * 31 + j + 1])
                if c == 0:
                    buildD(0)
                for n in range(NT):
                    ts = slice(n * TOK, (n + 1) * TOK)
                    ba = nbank()
                    for kc in range(KC):
                        MM(banks[ba][:], wcv3[:, kc, 0:128], hT[:, kc, ts], kc == 0, kc == KC - 1,
                           r=[rcv, ("hT", kc, n)], w=[("ps", ba)])
                    bg = nbank()
                    for kc in range(KC):
                        MM(banks[bg][:], wcv3[:, kc, 128:256], hT[:, kc, ts], kc == 0, kc == KC - 1,
                           r=[rcv, ("hT", kc, n)], w=[("ps", bg)])
                    i = k % 2
                    k += 1
                    ACT(ge[i], banks[bg][:], AF.Exp, r=[("ps", bg)], w=[("c_e", i)], scale=-1.0)
                    ACT(ge[i], ge[i], AF.Ln, r=[("c_e", i)], w=[("c_e", i)], bias=1.0)
                    ACT(gs[i], ge[i], AF.Exp, r=[("c_e", i)], w=[("c_s", i)], scale=-1.0)
                    TT("dve", zp[:, 30 + n * TOK: 30 + (n + 1) * TOK], banks[ba][:], gs[i], ALU.mult,
                       r=[("ps", ba), ("c_s", i), ("zpad_z", c % 2)], w=[("zpad", c % 2, n)])
                WS.done(("cv", c))
                if c + 1 < 4:
                    buildD(c + 1)
                for n in range(NT):
                    bc = nbank()
                    rd = [("zpad", c % 2, n), ("D", c % 2), ("zpad_z", c % 2)] + ([("zpad", c % 2, n - 1)] if n else [])
                    for j in range(31):
                        MM(banks[bc][:], D[:, j, :], zp[:, n * TOK + j: n * TOK + j + TOK], j == 0, j == 30,
                           r=rd, w=[("ps", bc)])
                    ACT(cT[:, c, n * TOK:(n + 1) * TOK], banks[bc][:], AF.Identity, r=[("ps", bc), "par"],
                        w=[("cT", c, n)], bias=par_t[:, pc + P_DWB + c: pc + P_DWB + c + 1])
            P.barrier()
            L0 = 16384
            sq = [ab16(L0 + 4096 * i, 2048).rearrange("p (c t) -> p c t", c=4) for i in range(2)]
            m2 = [af32(L0 + 8192 + 2048 * i, 512) for i in range(2)]
            mean_s = [af32(L0 + 12288 + 2048 * i, 512) for i in range(4)]
            rstd = [af32(L0 + 20480 + 2048 * i, 512) for i in range(4)]
            tt = [af32(L0 + 28672 + 2048 * i, 512) for i in range(4)]
            for n in range(NT):
                ts = slice(n * TOK, (n + 1) * TOK)
                i2 = n % 2
                cr = [("cT", c, n) for c in range(4)]
                ACT(sq[i2], cT[:, :, ts], AF.Square, r=cr, w=[("l_sq", i2)])
                bm = nbank()
                for c in range(4):
                    MM(banks[bm][:], onesm, cT[:, c, ts], c == 0, c == 3, r=[("cT", c, n), "onesm"], w=[("ps", bm)])
                bq = nbank()
                for c in range(4):
                    MM(banks[bq][:], onesm, sq[i2][:, c, :], c == 0, c == 3, r=[("l_sq", i2), "onesm"],
                       w=[("ps", bq)])
                ACT(mean_s[n], banks[bm][:], AF.Identity, r=[("ps", bm)], w=[("l_mean", n)])
                ACT(m2[i2], banks[bm][:], AF.Square, r=[("ps", bm)], w=[("l_m2", i2)])
                TT("dve", m2[i2], banks[bq][:], m2[i2], ALU.subtract, r=[("ps", bq), ("l_m2", i2)], w=[("l_m2", i2)])
                ACT(rstd[n], m2[i2], AF.Ln, r=[("l_m2", i2)], w=[("l_rstd", n)], bias=EPS)
                ACT(rstd[n], rstd[n], AF.Exp, r=[("l_rstd", n)], w=[("l_rstd", n)], scale=-0.5)
            k = 0
            for n in range(NT):
                ts = slice(n * TOK, (n + 1) * TOK)
                for c in range(4):
                    i = k % 4
                    k += 1
                    TT("dve", tt[i], cT[:, c, ts], mean_s[n], ALU.subtract, r=[("cT", c, n), ("l_mean", n)],
                       w=[("l_t", i)])
                    TT("dve", tt[i], tt[i], rstd[n], ALU.mult, r=[("l_t", i), ("l_rstd", n)], w=[("l_t", i)])
                    ACT(cT[:, c, ts], tt[i], AF.Silu, r=[("l_t", i), "par"], w=[("cT", c, n)],
                        scale=par_t[:, pc + P_LNG + c: pc + P_LNG + c + 1],
                        bias=par_t[:, pc + P_LNB + c: pc + P_LNB + c + 1])

        def merge_out(l):
            cT = ab16(0, 8192).rearrange("p (c t) -> p c t", c=4)
            mT = ab16(16384, 8192).rearrange("p (f t) -> p f t", f=8)
            M0 = 32768
            e_ = [af32(M0, 512), af32(M0 + 2048, 512)]
            s_ = [af32(M0 + 4096, 512), af32(M0 + 6144, 512)]
            t_ = [af32(M0 + 8192, 512), af32(M0 + 10240, 512)]
            k = 0
            for th in range(2):
                for f in range(8):
                    wg, rg = WS.get(("g", f))
                    wc, rc = WS.get(("wc", f))
                    wg3 = wg.rearrange("p (k c) -> p k c", k=KC)
                    wc3 = wc.rearrange("p (k c) -> p k c", k=4)
                    for nn in range(2):
                        n = th * 2 + nn
                        ts = slice(n * TOK, (n + 1) * TOK)
                        byc, bya, bgc, bga = nbank(), nbank(), nbank(), nbank()
                        for c in range(4):
                            MM(banks[byc][:], wc3[:, c, 0:128], cT[:, c, ts], c == 0, c == 3,
                               r=[rc, ("cT", c, n)], w=[("ps", byc)])
                        for c in range(4):
                            MM(banks[bya][:], wc3[:, c, 128:256], oT[:, c, ts], c == 0, c == 3,
                               r=[rc, ("oT", c)], w=[("ps", bya)])
                        for kc in range(KC):
                            MM(banks[bgc][:], wg3[:, kc, 0:128], hT[:, kc, ts], kc == 0, kc == KC - 1,
                               r=[rg, ("hT", kc, n)], w=[("ps", bgc)])
                        for kc in range(KC):
                            MM(banks[bga][:], wg3[:, kc, 128:256], hT[:, kc, ts], kc == 0, kc == KC - 1,
                               r=[rg, ("hT", kc, n)], w=[("ps", bga)])
                        for ii, (bgx, byx) in enumerate(((bgc, byc), (bga, bya))):
                            ACT(e_[ii], banks[bgx][:], AF.Exp, r=[("ps", bgx)], w=[("m_e", ii)], scale=-1.0)
                            ACT(e_[ii], e_[ii], AF.Ln, r=[("m_e", ii)], w=[("m_e", ii)], bias=1.0)
                            ACT(s_[ii], e_[ii], AF.Exp, r=[("m_e", ii)], w=[("m_s", ii)], scale=-1.0)
                            TT("dve", t_[ii], banks[byx][:], s_[ii], ALU.mult, r=[("ps", byx), ("m_s", ii)],
                               w=[("m_t", ii)])
                        TT("dve", mT[:, f, nn * TOK:(nn + 1) * TOK], t_[0], t_[1], ALU.add,
                           r=[("m_t", 0), ("m_t", 1)], w=[("mT", f, nn)])
                    WS.done(("g", f))
                    WS.done(("wc", f))
                for f in range(8):
                    wo, ro = WS.get(("wo", f))
                    wo3 = wo.rearrange("p (k c) -> p k c", k=KC)
                    for nn in range(2):
                        n = th * 2 + nn
                        ts = slice(n * TOK, (n + 1) * TOK)
                        b = nbank()
                        for kc in range(KC):
                            MM(banks[b][:], wo3[:, kc, :], mT[:, kc, nn * TOK:(nn + 1) * TOK], kc == 0, kc == KC - 1,
                               r=[ro, ("mT", kc, nn)], w=[("ps", b)])
                        TT("dve", xT[:, f, ts], banks[b][:], xT[:, f, ts], ALU.add, r=[("ps", b), ("xT", f, n)],
                           w=[("xT", f, n)])
                    WS.done(("wo", f))
                if th == 0:
                    rmsnorm(l, P_G2, False, chunks_=(0, 1))

        def ffn(l):
            h1 = ab16(0, 16384).rearrange("p (c t) -> p c t", c=8)
            F0 = 32768
            rr = [af32(F0, 512), af32(F0 + 2048, 512), af32(F0 + 4096, 512)]
            k = 0
            for qd in range(4):
                for cc in range(4):
                    w1, r1 = WS.get(("f1", qd, cc))
                    w13 = w1.rearrange("p (k c) -> p k c", k=KC)
                    for c2 in range(2):
                        ci = cc * 2 + c2
                        for n in range(NT):
                            ts = slice(n * TOK, (n + 1) * TOK)
                            b = nbank()
                            for kc in range(KC):
                                MM(banks[b][:], w13[:, kc, c2 * 128:(c2 + 1) * 128], hT[:, kc, ts],
                                   kc == 0, kc == KC - 1, r=[r1, ("hT", kc, n)], w=[("ps", b)])
                            i = k % 3
                            k += 1
                            ACT(rr[i], banks[b][:], AF.Relu, r=[("ps", b)], w=[("f_r", i)])
                            TT("dve", h1[:, ci, ts], rr[i], rr[i], ALU.mult, r=[("f_r", i)], w=[("h1", ci, n)])
                    WS.done(("f1", qd, cc))
                halves = [(0, 1), (2, 3)] if qd == 3 else [(0, 1, 2, 3)]
                for hi_, nset in enumerate(halves):
                    for jj in range(4):
                        w2, r2 = WS.get(("f2", qd, jj))
                        w24 = w2.rearrange("p (j k c) -> p j k c", j=2, k=8)
                        for j2 in range(2):
                            j = jj * 2 + j2
                            for n in nset:
                                ts = slice(n * TOK, (n + 1) * TOK)
                                b = nbank()
                                for hc in range(8):
                                    MM(banks[b][:], w24[:, j2, hc, :], h1[:, hc, ts], hc == 0, hc == 7,
                                       r=[r2, ("h1", hc, n)], w=[("ps", b)])
                                TT("dve", xT[:, j, ts], banks[b][:], xT[:, j, ts], ALU.add,
                                   r=[("ps", b), ("xT", j, n)], w=[("xT", j, n)])
                        WS.done(("f2", qd, jj))
                    if qd == 3 and hi_ == 0 and l + 1 < L:
                        rmsnorm(l + 1, P_G1, False, chunks_=(0, 1))

        import os as _os
        _stop = int(_os.environ.get("KSTOP", "99"))
        for l in range(L):
            if _stop >= 1:
                rmsnorm(l, P_G1, first=(l == 0), chunks_=(0, 1, 2, 3) if l == 0 else (2, 3))
                P.barrier()
            if _stop >= 2:
                attention(l)
                P.barrier()
            if _stop >= 3:
                convbranch(l)
                P.barrier()
            if _stop >= 4:
                merge_out(l)
                P.barrier()
            if _stop >= 5:
                rmsnorm(l, P_G2, first=False, chunks_=(2, 3))
                P.barrier()
            if _stop >= 6:
                ffn(l)
                P.barrier()

        if dbg is not None:
            src = dbg[0](locals())
            P.op("sp", lambda e: e.dma_start(out=dbg_out[:, :], in_=src), r=[], w=["dbgout"], dma="dbg")
            P.op("sp", lambda e: e.nop(), r=["dbgout"])
        for i in range(4):
            P.op("sp", lambda e, i=i: e.dma_start(out=yout[:, i * 4096:(i + 1) * 4096],
                                                   in_=xT_t[:, i * 4096:(i + 1) * 4096]),
                 r=[("xT", kc, n) for kc in (2 * i, 2 * i + 1) for n in range(NT)], w=["yout"], dma="st")
        P.op("sp", lambda e: e.nop(), r=["yout"])
        P.emit(sems, blk)
    return nc


_NC_CACHE = {}


def _get_nc(L):
    if L not in _NC_CACHE:
        _NC_CACHE[L] = build_nc(L)
    return _NC_CACHE[L]


def _x_to_dev(xb):
    return np.ascontiguousarray(xb.T.reshape(KC, 128, SEQ).transpose(1, 0, 2).reshape(128, KC * SEQ))


def _x_from_dev(y):
    return np.ascontiguousarray(y.reshape(128, KC, SEQ).transpose(1, 0, 2).reshape(D_MODEL, SEQ).T)


FUSED = True


def kernel(**inputs):
    inp = {k: np.asarray(v, dtype=np.float32) for k, v in inputs.items()}
    x = inp["x"]
    B = x.shape[0]
    btab, mtab = bias_tables(inp["rel_bias"])
    cst = const_table()
    xdev = [_x_to_dev(x[b]) for b in range(B)]
    groups = [list(range(DEPTH))] if FUSED else [[l] for l in range(DEPTH)]
    for layers in groups:
        Lg = len(layers)
        nc = _get_nc(Lg)
        wpk = np.stack([pack_layer(inp, l) for l in layers], axis=0)
        ppk = np.concatenate([pack_params(inp, l) for l in layers], axis=1)
        in_maps = [{"xin": xdev[b], "wl": wpk, "par": ppk, "btab": btab, "mtab": mtab, "cst": cst}
                   for b in range(B)]
        res = run_bass_kernel_spmd(nc, in_maps, core_ids=list(range(B)))
        xdev = [np.asarray(res.results[b]["yout"], dtype=np.float32) for b in range(B)]
    out = np.stack([_x_from_dev(xdev[b]) for b in range(B)], axis=0)
    return out.astype(np.float32)
```

```python
from contextlib import ExitStack

import numpy as np
import concourse.bass as bass
import concourse.mybir as mybir
from concourse.bass_utils import run_bass_kernel_spmd

F32 = mybir.dt.float32
BF16 = mybir.dt.bfloat16
AF = mybir.ActivationFunctionType
ALU = mybir.AluOpType

D_MODEL = 1024
SEQ = 2048
DEPTH = 4
NCORES = 8
EPS = 1e-6
NT = 4
TOK = 512
KC = 8
DIL = (1, 4, 16)
NPAR = 154
RING_N = 5
RING_SZ = 2048


class Prog:
    STREAMS = ("pe", "act", "dve", "pool", "sp")
    NOSELF = ("pe", "sp", "act", "dve")

    def __init__(self):
        self.streams = {s: [] for s in self.STREAMS}
        self.last_w = {}
        self.readers = {}
        self.dma_domains = []
        self.pending = {}

    def op(self, stream, fn, r=(), w=(), dma=None):
        ops = self.streams[stream]
        me = (stream, len(ops))
        deps = set()
        for res in r:
            lw = self.last_w.get(res)
            if lw is not None:
                deps.add(lw)
        for res in w:
            lw = self.last_w.get(res)
            if lw is not None:
                deps.add(lw)
            for rd in self.readers.get(res, ()):
                deps.add(rd)
        pend = self.pending.pop(stream, None)
        if pend:
            deps |= pend
        deps.discard(me)
        best = {}
        red = set()
        for (ps_, pi_) in deps:
            if self.streams[ps_][pi_]["dma"] is not None:
                red.add((ps_, pi_))
            elif pi_ > best.get(ps_, -1):
                best[ps_] = pi_
        for ps_, pi_ in best.items():
            red.add((ps_, pi_))
        deps = red
        for res in r:
            self.readers.setdefault(res, []).append(me)
        for res in w:
            self.last_w[res] = me
            self.readers[res] = []
        if dma is not None and dma not in self.dma_domains:
            self.dma_domains.append(dma)
        ops.append(dict(fn=fn, deps=deps, dma=dma, sig=False, val=None))
        return me

    def barrier(self, streams=("pe", "act", "dve")):
        pend = set()
        for s in streams:
            n = len(self.streams[s])
            if n:
                pend.add((s, n - 1))
        for s in streams:
            self.pending[s] = set(pend) | self.pending.get(s, set())

    def resolve(self):
        for s, ops in self.streams.items():
            for o in ops:
                for (ps, pi) in o["deps"]:
                    p = self.streams[ps][pi]
                    if p["dma"] is None:
                        if ps == s and s in self.NOSELF:
                            continue
                        p["sig"] = True
        dcount = {}
        for s, ops in self.streams.items():
            cnt = 0
            for o in ops:
                if o["dma"] is not None:
                    d = o["dma"]
                    dcount[d] = dcount.get(d, 0) + 1
                    o["val"] = 16 * dcount[d]
                elif o["sig"]:
                    cnt += 1
                    o["val"] = cnt

    def emit(self, sems, block):
        self.resolve()
        prog = self

        def mk(stream):
            def body(eng):
                waited = {}
                for o in prog.streams[stream]:
                    need = {}
                    for (ps, pi) in o["deps"]:
                        p = prog.streams[ps][pi]
                        if p["dma"] is not None:
                            dom = p["dma"]
                        else:
                            if ps == stream and stream in prog.NOSELF:
                                continue
                            dom = ps
                        v = p["val"]
                        if v > need.get(dom, 0):
                            need[dom] = v
                    for dom, v in need.items():
                        if v > waited.get(dom, 0):
                            eng.wait_ge(sems[dom], v)
                            waited[dom] = v
                    ins = o["fn"](eng)
                    if o["dma"] is not None:
                        ins.then_inc(sems[o["dma"]], 16)
                    elif o["sig"]:
                        ins.then_inc(sems[stream], 1)
            return body

        block.tensor(mk("pe"))
        block.scalar(mk("act"))
        block.vector(mk("dve"))
        block.gpsimd(mk("pool"))
        block.sync(mk("sp"))


def _tile_list():
    tl = []
    for c in range(4):
        tl.append((("cv", c), 2048))
    for pi in range(4):
        for g in range(3):
            tl.append((("qk", pi, g), 2048))
            tl.append((("v", pi, g), 1024))
    for f in range(8):
        tl.append((("g", f), 2048))
        tl.append((("wc", f), 1024))
    for f in range(8):
        tl.append((("wo", f), 1024))
    for qd in range(4):
        for cc in range(4):
            tl.append((("f1", qd, cc), 2048))
        for jj in range(4):
            tl.append((("f2", qd, jj), 2048))
    return tl


TILES = _tile_list()
TILE_OFF = {}
_o = 0
for _k, _s in TILES:
    TILE_OFF[_k] = (_o, _s)
    _o += _s
WTOT = _o


def _kmaj(w):
    K, N = w.shape
    return w.reshape(K // 128, 128, N).transpose(1, 0, 2)


def pack_layer(inp, l):
    w_in = inp["w_in"][l]
    out = np.empty((128, WTOT), np.float32)

    def put(key, arr):
        o, s = TILE_OFF[key]
        out[:, o:o + s] = arr.reshape(128, s)

    for c in range(4):
        cols = np.r_[c * 128:(c + 1) * 128, 512 + c * 128:512 + (c + 1) * 128]
        put(("cv", c), _kmaj(w_in[:, cols]))
    for pi in range(4):
        for g in range(3):
            h0 = g * 8 + 2 * pi
            qc = 1024 + h0 * 64
            kc_ = 1024 + 1536 + h0 * 64
            vc = 1024 + 3072 + h0 * 64
            cols = np.r_[qc:qc + 128, kc_:kc_ + 128]
            put(("qk", pi, g), _kmaj(w_in[:, cols]))
            put(("v", pi, g), _kmaj(w_in[:, vc:vc + 128]))
    wco = inp["w_conv_out"][l]
    wao = inp["w_attn_out"][l]
    for f in range(8):
        cols = np.r_[5632 + f * 128:5632 + (f + 1) * 128, 6656 + f * 128:6656 + (f + 1) * 128]
        put(("g", f), _kmaj(w_in[:, cols]))
        a = _kmaj(wco[:, f * 128:(f + 1) * 128])
        b = _kmaj(wao[:, f * 128:(f + 1) * 128])
        put(("wc", f), np.concatenate([a, b], axis=2))
    wo = inp["w_out"][l]
    for f in range(8):
        put(("wo", f), _kmaj(wo[:, f * 128:(f + 1) * 128]))
    w1 = inp["w_ff1"][l]
    w2 = inp["w_ff2"][l]
    for qd in range(4):
        for cc in range(4):
            c0 = (qd * 8 + cc * 2) * 128
            put(("f1", qd, cc), _kmaj(w1[:, c0:c0 + 256]))
        rows = w2[qd * 1024:(qd + 1) * 1024]
        for jj in range(4):
            t = _kmaj(rows[:, jj * 256:(jj + 1) * 256])
            t = t.reshape(128, 8, 2, 128).transpose(0, 2, 1, 3)
            put(("f2", qd, jj), t)
    return out


def pack_params(inp, l):
    def fm(v, n):
        return v.reshape(n, 128).T
    cols = [
        fm(inp["norm1_g"][l], 8), fm(inp["norm2_g"][l], 8),
        inp["conv_dw_w"][l].reshape(31, 4, 128).transpose(2, 1, 0).reshape(128, 124),
        fm(inp["conv_dw_b"][l], 4), fm(inp["conv_ln_g"][l], 4), fm(inp["conv_ln_b"][l], 4),
        np.tile(inp["q_norm_g"][l], 2)[:, None], np.tile(inp["k_norm_g"][l], 2)[:, None],
    ]
    return np.concatenate(cols, axis=1).astype(np.float32)


P_G1, P_G2, P_DWW, P_DWB, P_LNG, P_LNB, P_GQ, P_GK = 0, 8, 16, 140, 144, 148, 152, 153


def _t5_bucket(dist):
    dist = np.asarray(dist, np.int64)
    nf = np.maximum(dist, 1).astype(np.float32)
    large = 16 + (np.log(nf / np.float32(16)) / np.float32(np.log(128.0)) * np.float32(16)).astype(np.int32)
    large = np.minimum(large, 31)
    return np.where(dist < 16, dist, large)


def bias_tables(rel_bias):
    k = np.arange(128)[:, None]
    q = np.arange(128)[None, :]
    off_prev = q + 128 - k
    off_cur = q - k
    m_prev = (off_prev <= 128).astype(np.float32)
    m_cur = (off_cur >= 0).astype(np.float32)
    bias = np.zeros((128, 12, 4, 128), np.float32)
    mask = np.zeros((128, 12, 4, 128), np.float32)
    for g in range(3):
        d = DIL[g]
        b_prev = _t5_bucket(np.clip(off_prev, 0, 128) * d)
        b_cur = _t5_bucket(np.clip(off_cur, 0, 128) * d)
        for pi in range(4):
            for hh in range(2):
                h = g * 8 + 2 * pi + hh
                bias[:, g * 4 + pi, 2 * hh + 0, :] = rel_bias[b_prev, h]
                bias[:, g * 4 + pi, 2 * hh + 1, :] = rel_bias[b_cur, h]
                mask[:, g * 4 + pi, 2 * hh + 0, :] = m_prev
                mask[:, g * 4 + pi, 2 * hh + 1, :] = m_cur
    return bias.reshape(128, 6144), mask.reshape(128, 6144)


def const_table():
    c = np.zeros((128, 512), np.float32)
    c[:, 0:128] = np.eye(128)
    c[0:64, 128:192] = 1.0
    c[64:128, 192:256] = 1.0
    c[:, 256:320] = 1.0
    c[:, 448:512] = 1.0
    return c


def tok_slice(g, bi):
    if g == 0:
        return slice(bi * 128, bi * 128 + 128, 1)
    if g == 1:
        r, n = bi // 4, bi % 4
        s = 512 * n + r
        return slice(s, s + 4 * 127 + 1, 4)
    return slice(bi, bi + 16 * 127 + 1, 16)


def has_prev(g, bi):
    if g == 0:
        return bi > 0
    if g == 1:
        return (bi % 4) > 0
    return False


def tok_chunks(g, bi):
    if g == 0:
        return [bi // 4]
    if g == 1:
        return [bi % 4]
    return [0, 1, 2, 3]


def build_nc(L, dbg=None):
    nc = bass.Bass("TRN2", target_bir_lowering=False)
    xin = nc.dram_tensor("xin", [128, KC * SEQ], F32, kind="ExternalInput").ap()
    wl = nc.dram_tensor("wl", [L, 128, WTOT], F32, kind="ExternalInput").ap()
    par = nc.dram_tensor("par", [128, L * NPAR], F32, kind="ExternalInput").ap()
    btab = nc.dram_tensor("btab", [128, 6144], F32, kind="ExternalInput").ap()
    mtab = nc.dram_tensor("mtab", [128, 6144], F32, kind="ExternalInput").ap()
    cst = nc.dram_tensor("cst", [128, 512], F32, kind="ExternalInput").ap()
    yout = nc.dram_tensor("yout", [128, KC * SEQ], F32, kind="ExternalOutput").ap()
    dbg_out = None
    if dbg is not None:
        dbg_out = nc.dram_tensor("dbg", [128, dbg[1]], F32, kind="ExternalOutput").ap()

    with ExitStack() as es:
        def sb(name, shape, dt):
            return es.enter_context(nc.sbuf_tensor(name, shape, dt))

        xT_t = sb("xT", [128, KC * SEQ], F32)
        hT_t = sb("hT", [128, KC * SEQ], BF16)
        oT_t = sb("oT", [128, 4 * SEQ], BF16)
        ring_t = sb("ring", [128, RING_N * RING_SZ], BF16)
        expb_t = sb("expb", [128, 6144], BF16)
        cst_t = sb("cstb", [128, 512], BF16)
        onesb_t = sb("onesb", [128, 128], BF16)
        onesm_t = sb("onesm", [128, 128], BF16)
        par_t = sb("par_sb", [128, L * NPAR], F32)
        npar_t = sb("npar_sb", [128, L * 8], F32)
        junk_t = sb("junk", [128, 8], F32)
        junk2_t = sb("junk2", [128, 8 * RING_N], F32)
        ARENA_W = 14592
        arena = sb("arena", [128, ARENA_W], F32)
        banks = [es.enter_context(nc.psum_tensor(f"bank{i}", [128, 512], F32)) for i in range(8)]

        P = Prog()
        sems = {}
        dma_doms = [f"ring{i}" for i in range(RING_N)] + [f"free{i}" for i in range(RING_N)] + ["ld_x0", "ld_x1", "ld_x2", "ld_x3", "ld_par", "ld_b", "ld_m", "ld_c", "st", "dbg", "shf", "shf2"]
        for d in list(Prog.STREAMS) + dma_doms:
            sems[d] = es.enter_context(nc.semaphore("s_" + d))
        blk = es.enter_context(nc.Block())

        xT = xT_t[:].rearrange("p (k t) -> p k t", k=KC)
        hT = hT_t[:].rearrange("p (k t) -> p k t", k=KC)
        oT = oT_t[:].rearrange("p (k t) -> p k t", k=4)
        ident = cst_t[:, 0:128]
        bdiag = cst_t[:, 128:256]
        onesA = cst_t[:, 256:384]
        onesB = cst_t[:, 384:512]
        onesb = onesb_t[:]
        onesm = onesm_t[:]
        expb = expb_t[:].rearrange("p (a t q) -> p a t q", a=12, t=4)

        def af32(off, n):
            return arena[:, off // 4: off // 4 + n]

        def ab16(off, n):
            return arena[:, off // 4: off // 4 + n // 2].bitcast(BF16)

        bstate = {"i": 0, "n": 8}

        def nbank():
            b = bstate["i"] % bstate["n"]
            bstate["i"] += 1
            return b

        def MM(out, lhsT, rhs, start, stop, r, w):
            P.op("pe", lambda e: e.matmul(out, lhsT=lhsT, rhs=rhs, start=start, stop=stop), r=r, w=w)

        def ACT(out, in_, func, r, w, scale=1.0, bias=0.0):
            P.op("act", lambda e: e.activation(out=out, in_=in_, func=func, scale=scale, bias=bias), r=r, w=w)

        def TT(eng, out, in0, in1, op, r, w):
            P.op(eng, lambda e: e.tensor_tensor(out=out, in0=in0, in1=in1, op=op), r=r, w=w)

        def STT(out, in0, scalar, in1, op0, op1, r, w):
            P.op("dve", lambda e: e.scalar_tensor_tensor(out=out, in0=in0, scalar=scalar, in1=in1,
                                                         op0=op0, op1=op1), r=r, w=w)

        def TS(eng, out, in0, s1, op0, r, w, s2=None, op1=None):
            if op1 is None:
                P.op(eng, lambda e: e.tensor_scalar(out=out, in0=in0, scalar1=s1, scalar2=None, op0=op0), r=r, w=w)
            else:
                P.op(eng, lambda e: e.tensor_scalar(out=out, in0=in0, scalar1=s1, scalar2=s2, op0=op0, op1=op1),
                     r=r, w=w)

        def CP(eng, out, in_, r, w):
            if eng == "act":
                P.op(eng, lambda e: e.activation(out=out, in_=in_, func=AF.Identity), r=r, w=w)
            else:
                P.op(eng, lambda e: e.tensor_copy(out=out, in_=in_), r=r, w=w)

        def MEMSET(eng, ap, val, w):
            P.op(eng, lambda e: e.memset(ap, val), w=w)

        ring_state = {"i": 0}

        def wload(l, key):
            s = ring_state["i"] % RING_N
            ring_state["i"] += 1
            off, size = TILE_OFF[key]
            dst = ring_t[:, s * RING_SZ: s * RING_SZ + size]
            src = wl[l, :, off:off + size]
            if P.readers.get(("ring", s)):
                P.op("sp", lambda e: e.dma_start(out=junk2_t[:, s * 8:(s + 1) * 8], in_=junk_t[:, :]), r=["junk"],
                     w=[("ring", s)], dma=f"free{s}")
            P.op("pool", lambda e: e.dma_start(out=dst, in_=src), w=[("ring", s)], dma=f"ring{s}")
            return dst, ("ring", s)

        class WStream:
            def __init__(self, seq):
                self.seq = seq
                self.pos = 0
                self.head = 0
                self.loaded = {}
                self.busy = [None] * RING_N
                self.key2idx = {}

            def _fill(self):
                while self.pos < len(self.seq):
                    s_ = self.pos % RING_N
                    if self.busy[s_] is not None:
                        break
                    l, key = self.seq[self.pos]
                    self.busy[s_] = self.pos
                    self.loaded[self.pos] = wload(l, key)
                    self.pos += 1

            def get(self, key):
                self._fill()
                i = self.head
                assert self.seq[i][1] == key, (self.seq[i], key)
                assert i in self.loaded, "ring too small for outstanding tiles"
                self.head += 1
                self.key2idx[key] = i
                return self.loaded.pop(i)

            def done(self, key):
                i = self.key2idx.pop(key)
                s_ = i % RING_N
                assert self.busy[s_] == i
                self.busy[s_] = None
                self._fill()

        wseq = []
        for l in range(L):
            for pi in range(4):
                for g in range(3):
                    wseq.append((l, ("qk", pi, g)))
                    wseq.append((l, ("v", pi, g)))
            for c in range(4):
                wseq.append((l, ("cv", c)))
            for th in range(2):
                for f in range(8):
                    wseq.append((l, ("g", f)))
                    wseq.append((l, ("wc", f)))
                for f in range(8):
                    wseq.append((l, ("wo", f)))
            for qd in range(4):
                for cc in range(4):
                    wseq.append((l, ("f1", qd, cc)))
                for rep in range(2 if qd == 3 else 1):
                    for jj in range(4):
                        wseq.append((l, ("f2", qd, jj)))
        WS = WStream(wseq)

        for i in range(4):
            P.op("sp", lambda e, i=i: e.dma_start(out=xT_t[:, i * 4096:(i + 1) * 4096],
                                                   in_=xin[:, i * 4096:(i + 1) * 4096]),
                 w=[("xT", kc, n) for kc in (2 * i, 2 * i + 1) for n in range(NT)], dma=f"ld_x{i}")
        P.op("sp", lambda e: e.dma_start(out=par_t[:], in_=par[:, :]), w=["par"], dma="ld_par")
        P.op("pool", lambda e: e.dma_start(out=cst_t[:], in_=cst[:, :]), w=["cst"], dma="ld_c")
        bt = af32(0, 6144)
        mt = af32(24576, 6144)
        P.op("sp", lambda e: e.dma_start(out=bt, in_=btab[:, :]), w=["bt"], dma="ld_b")
        P.op("sp", lambda e: e.dma_start(out=mt, in_=mtab[:, :]), w=["mt"], dma="ld_m")
        MEMSET("dve", onesb, 1.0, w=["onesb"])
        MEMSET("dve", junk_t[:], 0.0, w=["junk"])
        MEMSET("dve", onesm, 1.0 / 512.0, w=["onesm"])
        for i in range(3):
            sl_ = slice(i * 2048, (i + 1) * 2048)
            TS("dve", mt[:, sl_], mt[:, sl_], 240000.0, ALU.mult, r=["mt"], w=["mt"], s2=-240000.0, op1=ALU.add)
            STT(expb_t[:, sl_], bt[:, sl_], 8.0, mt[:, sl_], ALU.mult, ALU.add, r=["bt", "mt"], w=["expb"])
        for l in range(L):
            TS("dve", npar_t[:, l * 8:l * 8 + 8], par_t[:, l * NPAR + P_LNG: l * NPAR + P_LNG + 8], -1.0,
               ALU.mult, r=["par"], w=["npar"])
        P.barrier()

        def rmsnorm(l, gcol, first, chunks_=(0, 1, 2, 3)):
            sq = ab16(45056, 4096).rearrange("p (k t) -> p k t", k=KC)
            lnv = af32(53248, 512)
            rstd = af32(55296, 512)
            for n in chunks_:
                ts = slice(n * TOK, (n + 1) * TOK)
                xr = [("xT", kc, n) for kc in range(KC)]
                ACT(sq, xT[:, :, ts], AF.Square, r=xr, w=["n_sq"])
                b = nbank()
                for kc in range(KC):
                    MM(banks[b][:], onesb, sq[:, kc, :], kc == 0, kc == KC - 1, r=["n_sq", "onesb"], w=[("ps", b)])
                ACT(lnv, banks[b][:], AF.Ln, r=[("ps", b)], w=["n_ln"], scale=1.0 / D_MODEL, bias=EPS)
                ACT(rstd, lnv, AF.Exp, r=["n_ln"], w=["n_rstd"], scale=-0.5)
                for kc in range(KC):
                    STT(hT[:, kc, ts], xT[:, kc, ts], par_t[:, l * NPAR + gcol + kc: l * NPAR + gcol + kc + 1], rstd,
                        ALU.mult, ALU.mult, r=[("xT", kc, n), "n_rstd", "par"],
                        w=[("hT", kc, n)])

        def attention(l):
            bstate["n"] = 4
            qpA = ab16(0, 2048)
            qpB = ab16(4096, 2048)
            kT = ab16(8192, 2048)
            VpA = ab16(12288, 2048).rearrange("p (b d) -> p b d", b=16)
            VpB = ab16(16384, 2048).rearrange("p (b d) -> p b d", b=16)
            numacc = af32(20480, 2048)
            denacc = af32(28672, 2048)
            S0 = 36864
            uqs = [af32(S0 + 2048 * i, 512) for i in range(3)]
            sqb = [ab16(S0 + 6144 + 1024 * i, 512) for i in range(3)]
            lnv = [af32(S0 + 9216 + 2048 * i, 512) for i in range(3)]
            rstd = lnv
            eb = [ab16(S0 + 15360 + 1024 * i, 512) for i in range(2)]
            pTb = [ab16(S0 + 17408 + 1024 * i, 512) for i in range(4)]
            MEMSET("dve", qpA[64:128, :], 0.0, w=["qpA_z"])
            MEMSET("dve", qpB[0:64, :], 0.0, w=["qpB_z"])
            MEMSET("dve", VpA[:, :, 64:128], 1.0, w=["VpA_z"])
            MEMSET("dve", VpB[:, :, 0:64], 1.0, w=["VpB_z"])
            import os as _os2
            ksub = int(_os2.environ.get("KSUB", "99"))
            cnt = {"u": 0, "e": 0, "p": 0}
            gq = par_t[:, l * NPAR + P_GQ: l * NPAR + P_GQ + 1]
            gk = par_t[:, l * NPAR + P_GK: l * NPAR + P_GK + 1]
            TMPRES = [("a_uq", i) for i in range(3)] + [("a_sq", i) for i in range(3)]
            NACC = [("numacc", q_) for q_ in range(4)]
            DACC = [("denacc", q_) for q_ in range(4)]
            tmpd = af32(S0, 2048)

            def normalise_dma(pj):
                ia = P.op("sp", lambda e: e.dma_start(out=tmpd[0:64, :], in_=numacc[64:128, :]), r=NACC + DACC,
                          w=TMPRES + [("tmpd", q_) for q_ in range(4)], dma="shf")
                depsA = set(P.streams["sp"][ia[1]]["deps"])
                ib = P.op("sp", lambda e: e.dma_start(out=tmpd[64:128, :], in_=denacc[0:64, :]), r=[],
                          w=[("tmpdB", q_) for q_ in range(4)], dma="shf2")
                P.streams["sp"][ib[1]]["deps"] |= depsA

            def normalise_chunk(pj, q_):
                cs = slice(q_ * 512, (q_ + 1) * 512)
                tr = [("tmpd", q_), ("tmpdB", q_)]
                ACT(tmpd[:, cs], tmpd[:, cs], AF.Ln, r=tr + TMPRES, w=tr)
                ACT(tmpd[:, cs], tmpd[:, cs], AF.Exp, r=tr + TMPRES, w=tr, scale=-1.0)
                TT("dve", oT[0:64, pj, cs], numacc[0:64, cs], tmpd[0:64, cs], ALU.mult,
                   r=[("numacc", q_)] + tr + TMPRES, w=[("oT", pj)])
                TT("dve", oT[64:128, pj, cs], denacc[64:128, cs], tmpd[64:128, cs], ALU.mult,
                   r=[("denacc", q_)] + tr + TMPRES, w=[("oT", pj)])

            pend_norm = {"pi": None}
            kpg = int(_os2.environ.get("KPG", "99"))
            for pi in range(4):
                for g in range(3):
                    if pi * 3 + g >= kpg:
                        WS.get(("qk", pi, g)); WS.get(("v", pi, g))
                        WS.done(("qk", pi, g)); WS.done(("v", pi, g))
                        continue
                    wqk, rqk = WS.get(("qk", pi, g))
                    wv, rv = WS.get(("v", pi, g))
                    wqk3 = wqk.rearrange("p (k c) -> p k c", k=KC)
                    wv3 = wv.rearrange("p (k c) -> p k c", k=KC)
                    chunks = [(which, n) for which in range(2) for n in range(NT)]
                    cst_ = {}

                    def stageP(ci):
                        which, n = chunks[ci]
                        ts = slice(n * TOK, (n + 1) * TOK)
                        b = nbank()
                        for kc in range(KC):
                            MM(banks[b][:], wqk3[:, kc, which * 128:(which + 1) * 128], hT[:, kc, ts],
                               kc == 0, kc == KC - 1, r=[rqk, ("hT", kc, n)], w=[("ps", b)])
                        u = cnt["u"] % 3
                        cnt["u"] += 1
                        cst_[ci] = u
                        CP("dve", uqs[u], banks[b][:], r=[("ps", b)], w=[("a_uq", u)])
                        ACT(sqb[u], uqs[u], AF.Square, r=[("a_uq", u)], w=[("a_sq", u)])

                    def stageN(ci):
                        which, n = chunks[ci]
                        ts = slice(n * TOK, (n + 1) * TOK)
                        u = cst_[ci]
                        b2 = nbank()
                        MM(banks[b2][:], bdiag, sqb[u], True, True, r=[("a_sq", u), "cst"], w=[("ps", b2)])
                        ACT(lnv[u], banks[b2][:], AF.Ln, r=[("ps", b2)], w=[("a_rstd", u)], scale=1.0 / 64.0, bias=EPS)
                        ACT(rstd[u], lnv[u], AF.Exp, r=[("a_rstd", u)], w=[("a_rstd", u)], scale=-0.5)
                        if which == 0:
                            STT(qpA[0:64, ts], uqs[u][0:64, :], gq[0:64, :], rstd[u][0:64, :], ALU.mult, ALU.mult,
                                r=[("a_uq", u), ("a_rstd", u), "par", "qpA_z"], w=[("qpA", n)])
                            STT(qpB[64:128, ts], uqs[u][64:128, :], gq[64:128, :], rstd[u][64:128, :],
                                ALU.mult, ALU.mult,
                                r=[("a_uq", u), ("a_rstd", u), "par", "qpB_z"], w=[("qpB", n)])
                        else:
                            STT(kT[:, ts], uqs[u], gk, rstd[u], ALU.mult, ALU.mult,
                                r=[("a_uq", u), ("a_rstd", u), "par"], w=[("kT", n)])

                    def vproj(b4):
                        if True:
                            b = nbank()
                            for j in range(4):
                                bi = b4 * 4 + j
                                sl = tok_slice(g, bi)
                                for kc in range(KC):
                                    MM(banks[b][:, j * 128:(j + 1) * 128], hT[:, kc, sl], wv3[:, kc, :],
                                       kc == 0, kc == KC - 1,
                                       r=[rv] + [("hT", kc, n) for n in tok_chunks(g, bi)], w=[("ps", b)])
                            pv = banks[b][:].rearrange("p (j d) -> p j d", j=4)
                            CP("dve", VpA[:, b4 * 4:(b4 + 1) * 4, 0:64], pv[:, :, 0:64], r=[("ps", b), "VpA_z"],
                               w=[("VpA", b4)])
                            CP("dve", VpB[:, b4 * 4:(b4 + 1) * 4, 64:128], pv[:, :, 64:128], r=[("ps", b), "VpB_z"],
                               w=[("VpB", b4)])

                    VSCHED = {0: [], 1: [], 2: [0], 3: [1], 4: [2], 5: [3], 6: [], 7: []}
                    stageP(0)
                    stageP(1)
                    for ci in range(8):
                        if ci + 2 < 8:
                            stageP(ci + 2)
                        for vb in VSCHED[ci]:
                            vproj(vb)
                        if ci == 5:
                            WS.done(("qk", pi, g))
                        if ci == 7:
                            WS.done(("v", pi, g))
                        stageN(ci)

                    sst = {}

                    def stageS(bi):
                        hp = has_prev(g, bi)
                        qs = tok_slice(g, bi)
                        qch = tok_chunks(g, bi)
                        kbl = ([bi - 1] if hp else []) + [bi]
                        ntile = 2 * len(kbl)
                        sbk = nbank()
                        W = ntile * 128
                        if hp:
                            MM(banks[sbk][:, 0:W], ident, expb[:, g * 4 + pi, :, :].rearrange("p t q -> p (t q)"),
                               True, False, r=["cst", "expb"], w=[("ps", sbk)])
                        else:
                            MM(banks[sbk][:, 0:W].rearrange("p (t q) -> p t q", t=2), ident,
                               expb[:, g * 4 + pi, 1::2, :], True, False, r=["cst", "expb"], w=[("ps", sbk)])
                        t = 0
                        for hh in range(2):
                            qp = qpA if hh == 0 else qpB
                            qn = "qpA" if hh == 0 else "qpB"
                            for kb in kbl:
                                MM(banks[sbk][:, t * 128:(t + 1) * 128], kT[:, tok_slice(g, kb)], qp[:, qs],
                                   False, t == ntile - 1,
                                   r=[("kT", n) for n in tok_chunks(g, kb)] + [(qn, n) for n in qch],
                                   w=[("ps", sbk)])
                                t += 1
                        W = ntile * 128
                        pidx = cnt["p"] % 4
                        cnt["p"] += 1
                        pt = pTb[pidx]
                        ACT(pt[:, 0:W], banks[sbk][:, 0:W], AF.Exp, r=[("ps", sbk)], w=[("a_p", pidx)], scale=0.125)
                        sst[bi] = (kbl, pidx)

                    def stageV(bi):
                        u4, j = bi // 4, bi % 4
                        nb_, db_ = 4 + 2 * (u4 % 2), 5 + 2 * (u4 % 2)
                        kbl, pidx = sst.pop(bi)
                        pt = pTb[pidx]
                        nk = len(kbl)
                        for hh in range(2):
                            Vp = VpA if hh == 0 else VpB
                            vn = "VpA" if hh == 0 else "VpB"
                            bk = nb_ if hh == 0 else db_
                            for ki, kb in enumerate(kbl):
                                MM(banks[bk][:, j * 128:(j + 1) * 128], Vp[:, kb, :],
                                   pt[:, (hh * nk + ki) * 128:(hh * nk + ki + 1) * 128], ki == 0, ki == nk - 1,
                                   r=[(vn, kb // 4), ("a_p", pidx)], w=[("ps", bk)])
                        if j != 3:
                            return
                        if g == 0:
                            nv = numacc[:, u4 * 512:(u4 + 1) * 512]
                            dv = denacc[:, u4 * 512:(u4 + 1) * 512]
                            pn, pd = banks[nb_][:], banks[db_][:]
                        elif g == 1:
                            nv = numacc[:, u4::4]
                            dv = denacc[:, u4::4]
                            pn, pd = banks[nb_][:], banks[db_][:]
                        else:
                            nv = numacc[:, :].rearrange("p (q r) -> p r q", r=16)[:, u4 * 4:(u4 + 1) * 4, :]
                            dv = denacc[:, :].rearrange("p (q r) -> p r q", r=16)[:, u4 * 4:(u4 + 1) * 4, :]
                            pn = banks[nb_][:].rearrange("p (r q) -> p r q", r=4)
                            pd = banks[db_][:].rearrange("p (r q) -> p r q", r=4)
                        if g == 0:
                            if pend_norm["pi"] is not None:
                                normalise_chunk(pend_norm["pi"], u4)
                                if u4 == 3:
                                    pend_norm["pi"] = None
                            CP("act", nv, pn, r=[("ps", nb_)], w=[("numacc", u4)])
                            CP("dve", dv, pd, r=[("ps", db_)], w=[("denacc", u4)])
                        else:
                            TT("dve", nv, pn, nv, ALU.add, r=[("ps", nb_)] + NACC, w=NACC)
                            TT("dve", dv, pd, dv, ALU.add, r=[("ps", db_)] + DACC, w=DACC)

                    if pend_norm["pi"] is not None:
                        normalise_dma(pend_norm["pi"])
                    stageS(0)
                    stageS(1)
                    for bi in range(16):
                        if bi + 2 < 16:
                            stageS(bi + 2)
                        stageV(bi)
                pend_norm["pi"] = pi
            if pend_norm["pi"] is not None:
                normalise_dma(pend_norm["pi"])
                for q_ in range(4):
                    normalise_chunk(pend_norm["pi"], q_)
                pend_norm["pi"] = None
            bstate["n"] = 8

        def convbranch(l):
            cT = ab16(0, 8192).rearrange("p (c t) -> p c t", c=4)
            ZW = 2080
            zpad = [ab16(16384, ZW), ab16(16384 + 4160, ZW)]
            Dm = [ab16(24832, 31 * 128).rearrange("p (j m) -> p j m", j=31),
                  ab16(24832 + 7936, 31 * 128).rearrange("p (j m) -> p j m", j=31)]
            G0 = 40704
            ge = [af32(G0, 512), af32(G0 + 2048, 512)]
            gs = [af32(G0 + 4096, 512), af32(G0 + 6144, 512)]
            pc = l * NPAR
            for i in range(2):
                MEMSET("dve", zpad[i][:, 0:30], 0.0, w=[("zpad_z", i)])
            k = 0
            for c in range(4):
                wcv, rcv = WS.get(("cv", c))
                wcv3 = wcv.rearrange("p (k c) -> p k c", k=KC)
                zp = zpad[c % 2]
                D = Dm[c % 2]

                def buildD(cc_):
                    for j in range(31):
                        ACT(Dm[cc_ % 2][:, j, :], ident, AF.Identity, r=["cst", "par"], w=[("D", cc_ % 2)],
                            scale=par_t[:, pc + P_DWW + cc_ * 31 + j: pc + P_DWW + cc_ * 31 + j + 1])
                if c == 0:
                    buildD(0)
                for n in range(NT):
                    ts = slice(n * TOK, (n + 1) * TOK)
                    ba = nbank()
                    for kc in range(KC):
                        MM(banks[ba][:], wcv3[:, kc, 0:128], hT[:, kc, ts], kc == 0, kc == KC - 1,
                           r=[rcv, ("hT", kc, n)], w=[("ps", ba)])
                    bg = nbank()
                    for kc in range(KC):
                        MM(banks[bg][:], wcv3[:, kc, 128:256], hT[:, kc, ts], kc == 0, kc == KC - 1,
                           r=[rcv, ("hT", kc, n)], w=[("ps", bg)])
                    i = k % 2
                    k += 1
                    ACT(ge[i], banks[bg][:], AF.Exp, r=[("ps", bg)], w=[("c_e", i)], scale=-1.0)
                    ACT(ge[i], ge[i], AF.Ln, r=[("c_e", i)], w=[("c_e", i)], bias=1.0)
                    ACT(gs[i], ge[i], AF.Exp, r=[("c_e", i)], w=[("c_s", i)], scale=-1.0)
                    TT("dve", zp[:, 30 + n * TOK: 30 + (n + 1) * TOK], banks[ba][:], gs[i], ALU.mult,
                       r=[("ps", ba), ("c_s", i), ("zpad_z", c % 2)], w=[("zpad", c % 2, n)])
                WS.done(("cv", c))
                if c + 1 < 4:
                    buildD(c + 1)
                for n in range(NT):
                    bc = nbank()
                    rd = [("zpad", c % 2, n), ("D", c % 2), ("zpad_z", c % 2)] + ([("zpad", c % 2, n - 1)] if n else [])
                    for j in range(31):
                        MM(banks[bc][:], D[:, j, :], zp[:, n * TOK + j: n * TOK + j + TOK], j == 0, j == 30,
                           r=rd, w=[("ps", bc)])
                    ACT(cT[:, c, n * TOK:(n + 1) * TOK], banks[bc][:], AF.Identity, r=[("ps", bc), "par"],
                        w=[("cT", c, n)], bias=par_t[:, pc + P_DWB + c: pc + P_DWB + c + 1])
            P.barrier()
            L0 = 16384
            sq = [ab16(L0 + 4096 * i, 2048).rearrange("p (c t) -> p c t", c=4) for i in range(2)]
            m2 = [af32(L0 + 8192 + 2048 * i, 512) for i in range(2)]
            mean_s = [af32(L0 + 12288 + 2048 * i, 512) for i in range(4)]
            rstd = [af32(L0 + 20480 + 2048 * i, 512) for i in range(4)]
            tt = [af32(L0 + 28672 + 2048 * i, 512) for i in range(4)]
            for n in range(NT):
                ts = slice(n * TOK, (n + 1) * TOK)
                i2 = n % 2
                cr = [("cT", c, n) for c in range(4)]
                ACT(sq[i2], cT[:, :, ts], AF.Square, r=cr, w=[("l_sq", i2)])
                bm = nbank()
                for c in range(4):
                    MM(banks[bm][:], onesm, cT[:, c, ts], c == 0, c == 3, r=[("cT", c, n), "onesm"], w=[("ps", bm)])
                bq = nbank()
                for c in range(4):
                    MM(banks[bq][:], onesm, sq[i2][:, c, :], c == 0, c == 3, r=[("l_sq", i2), "onesm"],
                       w=[("ps", bq)])
                ACT(mean_s[n], banks[bm][:], AF.Identity, r=[("ps", bm)], w=[("l_mean", n)])
                ACT(m2[i2], banks[bm][:], AF.Square, r=[("ps", bm)], w=[("l_m2", i2)])
                TT("dve", m2[i2], banks[bq][:], m2[i2], ALU.subtract, r=[("ps", bq), ("l_m2", i2)], w=[("l_m2", i2)])
                ACT(rstd[n], m2[i2], AF.Ln, r=[("l_m2", i2)], w=[("l_rstd", n)], bias=EPS)
                ACT(rstd[n], rstd[n], AF.Exp, r=[("l_rstd", n)], w=[("l_rstd", n)], scale=-0.5)
            k = 0
            for n in range(NT):
                ts = slice(n * TOK, (n + 1) * TOK)
                for c in range(4):
                    i = k % 4
                    k += 1
                    TT("dve", tt[i], cT[:, c, ts], mean_s[n], ALU.subtract, r=[("cT", c, n), ("l_mean", n)],
                       w=[("l_t", i)])
                    TT("dve", tt[i], tt[i], rstd[n], ALU.mult, r=[("l_t", i), ("l_rstd", n)], w=[("l_t", i)])
                    ACT(cT[:, c, ts], tt[i], AF.Silu, r=[("l_t", i), "par"], w=[("cT", c, n)],
                        scale=par_t[:, pc + P_LNG + c: pc + P_LNG + c + 1],
                        bias=par_t[:, pc + P_LNB + c: pc + P_LNB + c + 1])

        def merge_out(l):
            cT = ab16(0, 8192).rearrange("p (c t) -> p c t", c=4)
            mT = ab16(16384, 8192).rearrange("p (f t) -> p f t", f=8)
            M0 = 32768
            e_ = [af32(M0, 512), af32(M0 + 2048, 512)]
            s_ = [af32(M0 + 4096, 512), af32(M0 + 6144, 512)]
            t_ = [af32(M0 + 8192, 512), af32(M0 + 10240, 512)]
            k = 0
            for th in range(2):
                for f in range(8):
                    wg, rg = WS.get(("g", f))
                    wc, rc = WS.get(("wc", f))
                    wg3 = wg.rearrange("p (k c) -> p k c", k=KC)
                    wc3 = wc.rearrange("p (k c) -> p k c", k=4)
                    for nn in range(2):
                        n = th * 2 + nn
                        ts = slice(n * TOK, (n + 1) * TOK)
                        byc, bya, bgc, bga = nbank(), nbank(), nbank(), nbank()
                        for c in range(4):
                            MM(banks[byc][:], wc3[:, c, 0:128], cT[:, c, ts], c == 0, c == 3,
                               r=[rc, ("cT", c, n)], w=[("ps", byc)])
                        for c in range(4):
                            MM(banks[bya][:], wc3[:, c, 128:256], oT[:, c, ts], c == 0, c == 3,
                               r=[rc, ("oT", c)], w=[("ps", bya)])
                        for kc in range(KC):
                            MM(banks[bgc][:], wg3[:, kc, 0:128], hT[:, kc, ts], kc == 0, kc == KC - 1,
                               r=[rg, ("hT", kc, n)], w=[("ps", bgc)])
                        for kc in range(KC):
                            MM(banks[bga][:], wg3[:, kc, 128:256], hT[:, kc, ts], kc == 0, kc == KC - 1,
                               r=[rg, ("hT", kc, n)], w=[("ps", bga)])
                        for ii, (bgx, byx) in enumerate(((bgc, byc), (bga, bya))):
                            ACT(e_[ii], banks[bgx][:], AF.Exp, r=[("ps", bgx)], w=[("m_e", ii)], scale=-1.0)
                            ACT(e_[ii], e_[ii], AF.Ln, r=[("m_e", ii)], w=[("m_e", ii)], bias=1.0)
                            ACT(s_[ii], e_[ii], AF.Exp, r=[("m_e", ii)], w=[("m_s", ii)], scale=-1.0)
                            TT("dve", t_[ii], banks[byx][:], s_[ii], ALU.mult, r=[("ps", byx), ("m_s", ii)],
                               w=[("m_t", ii)])
                        TT("dve", mT[:, f, nn * TOK:(nn + 1) * TOK], t_[0], t_[1], ALU.add,
                           r=[("m_t", 0), ("m_t", 1)], w=[("mT", f, nn)])
                    WS.done(("g", f))
                    WS.done(("wc", f))
                for f in range(8):
                    wo, ro = WS.get(("wo", f))
                    wo3 = wo.rearrange("p (k c) -> p k c", k=KC)
                    for nn in range(2):
                        n = th * 2 + nn
                        ts = slice(n * TOK, (n + 1) * TOK)
                        b = nbank()
                        for kc in range(KC):
                            MM(banks[b][:], wo3[:, kc, :], mT[:, kc, nn * TOK:(nn + 1) * TOK], kc == 0, kc == KC - 1,
                               r=[ro, ("mT", kc, nn)], w=[("ps", b)])
                        TT("dve", xT[:, f, ts], banks[b][:], xT[:, f, ts], ALU.add, r=[("ps", b), ("xT", f, n)],
                           w=[("xT", f, n)])
                    WS.done(("wo", f))
                if th == 0:
                    rmsnorm(l, P_G2, False, chunks_=(0, 1))

        def ffn(l):
            h1 = ab16(0, 16384).rearrange("p (c t) -> p c t", c=8)
            F0 = 32768
            rr = [af32(F0, 512), af32(F0 + 2048, 512), af32(F0 + 4096, 512)]
            k = 0
            for qd in range(4):
                for cc in range(4):
                    w1, r1 = WS.get(("f1", qd, cc))
                    w13 = w1.rearrange("p (k c) -> p k c", k=KC)
                    for c2 in range(2):
                        ci = cc * 2 + c2
                        for n in range(NT):
                            ts = slice(n * TOK, (n + 1) * TOK)
                            b = nbank()
                            for kc in range(KC):
                                MM(banks[b][:], w13[:, kc, c2 * 128:(c2 + 1) * 128], hT[:, kc, ts],
                                   kc == 0, kc == KC - 1, r=[r1, ("hT", kc, n)], w=[("ps", b)])
                            i = k % 3
                            k += 1
                            ACT(rr[i], banks[b][:], AF.Relu, r=[("ps", b)], w=[("f_r", i)])
                            TT("dve", h1[:, ci, ts], rr[i], rr[i], ALU.mult, r=[("f_r", i)], w=[("h1", ci, n)])
                    WS.done(("f1", qd, cc))
                halves = [(0, 1), (2, 3)] if qd == 3 else [(0, 1, 2, 3)]
                for hi_, nset in enumerate(halves):
                    for jj in range(4):
                        w2, r2 = WS.get(("f2", qd, jj))
                        w24 = w2.rearrange("p (j k c) -> p j k c", j=2, k=8)
                        for j2 in range(2):
                            j = jj * 2 + j2
                            for n in nset:
                                ts = slice(n * TOK, (n + 1) * TOK)
                                b = nbank()
                                for hc in range(8):
                                    MM(banks[b][:], w24[:, j2, hc, :], h1[:, hc, ts], hc == 0, hc == 7,
                                       r=[r2, ("h1", hc, n)], w=[("ps", b)])
                                TT("dve", xT[:, j, ts], banks[b][:], xT[:, j, ts], ALU.add,
                                   r=[("ps", b), ("xT", j, n)], w=[("xT", j, n)])
                        WS.done(("f2", qd, jj))
                    if qd == 3 and hi_ == 0 and l + 1 < L:
                        rmsnorm(l + 1, P_G1, False, chunks_=(0, 1))

        import os as _os
        _stop = int(_os.environ.get("KSTOP", "99"))
        for l in range(L):
            if _stop >= 1:
                rmsnorm(l, P_G1, first=(l == 0), chunks_=(0, 1, 2, 3) if l == 0 else (2, 3))
                P.barrier()
            if _stop >= 2:
                attention(l)
                P.barrier()
            if _stop >= 3:
                convbranch(l)
                P.barrier()
            if _stop >= 4:
                merge_out(l)
                P.barrier()
            if _stop >= 5:
                rmsnorm(l, P_G2, first=False, chunks_=(2, 3))
                P.barrier()
            if _stop >= 6:
                ffn(l)
                P.barrier()

        if dbg is not None:
            src = dbg[0](locals())
            P.op("sp", lambda e: e.dma_start(out=dbg_out[:, :], in_=src), r=[], w=["dbgout"], dma="dbg")
            P.op("sp", lambda e: e.nop(), r=["dbgout"])
        for i in range(4):
            P.op("sp", lambda e, i=i: e.dma_start(out=yout[:, i * 4096:(i + 1) * 4096],
                                                   in_=xT_t[:, i * 4096:(i + 1) * 4096]),
                 r=[("xT", kc, n) for kc in (2 * i, 2 * i + 1) for n in range(NT)], w=["yout"], dma="st")
        P.op("sp", lambda e: e.nop(), r=["yout"])
        P.emit(sems, blk)
    return nc


_NC_CACHE = {}


def _get_nc(L):
    if L not in _NC_CACHE:
        _NC_CACHE[L] = build_nc(L)
    return _NC_CACHE[L]


def _x_to_dev(xb):
    return np.ascontiguousarray(xb.T.reshape(KC, 128, SEQ).transpose(1, 0, 2).reshape(128, KC * SEQ))


def _x_from_dev(y):
    return np.ascontiguousarray(y.reshape(128, KC, SEQ).transpose(1, 0, 2).reshape(D_MODEL, SEQ).T)


FUSED = True


def kernel(**inputs):
    inp = {k: np.asarray(v, dtype=np.float32) for k, v in inputs.items()}
    x = inp["x"]
    B = x.shape[0]
    btab, mtab = bias_tables(inp["rel_bias"])
    cst = const_table()
    xdev = [_x_to_dev(x[b]) for b in range(B)]
    groups = [list(range(DEPTH))] if FUSED else [[l] for l in range(DEPTH)]
    for layers in groups:
        Lg = len(layers)
        nc = _get_nc(Lg)
        wpk = np.stack([pack_layer(inp, l) for l in layers], axis=0)
        ppk = np.concatenate([pack_params(inp, l) for l in layers], axis=1)
        in_maps = [{"xin": xdev[b], "wl": wpk, "par": ppk, "btab": btab, "mtab": mtab, "cst": cst}
                   for b in range(B)]
        res = run_bass_kernel_spmd(nc, in_maps, core_ids=list(range(B)))
        xdev = [np.asarray(res.results[b]["yout"], dtype=np.float32) for b in range(B)]
    out = np.stack([_x_from_dev(xdev[b]) for b in range(B)], axis=0)
    return out.astype(np.float32)
```
